# Optimizing a Trainium2 kernel written in Bass

```python
import math
import jax
import jax.numpy as jnp
from jax import lax
import numpy as np

D_MODEL = 4096
BATCH = 2
SEQ = 8192
DEPTH = 2

A_HEADS = 16
A_DK = 128
A_DV = 128
A_WIDTH = A_HEADS * A_DK
A_CONV = 4
A_CHUNK = 64
B_GROUPS = ((128, 1), (512, 4), (2048, 16))
B_HEADS_PER_GROUP = 8
B_HEAD_DIM = 128
B_HEADS = B_HEADS_PER_GROUP * len(B_GROUPS)
B_QKV_WIDTH = B_HEADS * B_HEAD_DIM
B_OUT_WIDTH = B_HEADS_PER_GROUP * B_HEAD_DIM
B_BLOCK = 128
ALIBI_MAX_BIAS = 8.0
C_CHUNK = 128
C_GROUPS = 16
C_GROUP_CH = 128
C_WIDTH = C_GROUPS * C_GROUP_CH
N_BRANCH = 3
IN_SIZES = (3 * A_WIDTH, A_HEADS, A_HEADS, A_HEADS * A_DV,
            3 * B_QKV_WIDTH, B_OUT_WIDTH,
            C_WIDTH, C_WIDTH, C_WIDTH,
            N_BRANCH * D_MODEL)
IN_WIDTH = sum(IN_SIZES)
NORM_EPS = 1e-6

kernel_name = 'hybrid_gdn_dilated_gmlp_block'


def rmsnorm(x, g):
    xf = x.astype(jnp.float32)
    y = xf * lax.rsqrt(jnp.mean(xf * xf, axis=-1, keepdims=True) + NORM_EPS)
    return (y * g.astype(jnp.float32)).astype(x.dtype)


def layernorm(x, g, b):
    xf = x.astype(jnp.float32)
    mu = jnp.mean(xf, axis=-1, keepdims=True)
    var = jnp.mean(jnp.square(xf - mu), axis=-1, keepdims=True)
    y = (xf - mu) * lax.rsqrt(var + NORM_EPS)
    return (y * g.astype(jnp.float32) + b.astype(jnp.float32)).astype(x.dtype)


def l2norm(x):
    return x * lax.rsqrt(jnp.sum(x * x, axis=-1, keepdims=True) + NORM_EPS)


def alibi_slopes(n):
    return 2.0 ** (-ALIBI_MAX_BIAS * jnp.arange(1, n + 1, dtype=jnp.float32) / n)


def causal_depthwise_conv(x, w):
    k, ch = w.shape
    return lax.conv_general_dilated(x, w.astype(x.dtype)[:, None, :], (1,), ((k - 1, 0),),
                                    dimension_numbers=('NWC', 'WIO', 'NWC'),
                                    feature_group_count=ch)


def chunk_gated_delta_rule(q, k, v, g, beta):
    bsz, seq, heads, dk = q.shape
    dv = v.shape[-1]
    c = A_CHUNK
    n = seq // c

    def chunks(t):
        t = t.reshape((bsz, n, c, heads) + t.shape[3:])
        return jnp.moveaxis(t, 3, 2)

    q, k, v, g, beta = [chunks(t) for t in (q, k, v, g, beta)]
    G = jnp.cumsum(g, axis=-1)
    tril = jnp.tril(jnp.ones((c, c), dtype=bool))
    strict = jnp.tril(jnp.ones((c, c), dtype=bool), -1)
    diff = G[..., :, None] - G[..., None, :]
    decay = jnp.where(tril, jnp.exp(jnp.where(tril, diff, 0.0)), 0.0)
    k_beta = k * beta[..., None]
    a_strict = jnp.where(strict, jnp.einsum('bnhid,bnhjd->bnhij', k_beta, k) * decay, 0.0)
    lower = a_strict + jnp.eye(c, dtype=a_strict.dtype)
    rhs = jnp.concatenate([v * beta[..., None], k_beta * jnp.exp(G)[..., None]], axis=-1)
    wy = lax.linalg.triangular_solve(lower, rhs, left_side=True, lower=True, unit_diagonal=True)
    u, k_cum = wy[..., :dv], wy[..., dv:]
    qk = jnp.where(tril, jnp.einsum('bnhid,bnhjd->bnhij', q, k) * decay, 0.0)
    q_dec = q * jnp.exp(G)[..., None]
    k_dec = k * jnp.exp(G[..., -1:] - G)[..., None]
    chunk_decay = jnp.exp(G[..., -1])

    def step(state, xs):
        u_n, kc_n, qk_n, qd_n, kd_n, cd_n = xs
        v_new = u_n - jnp.einsum('bhcd,bhde->bhce', kc_n, state)
        o_n = jnp.einsum('bhcd,bhde->bhce', qd_n, state) + jnp.einsum('bhij,bhje->bhie', qk_n, v_new)
        state = state * cd_n[..., None, None] + jnp.einsum('bhcd,bhce->bhde', kd_n, v_new)
        return state, o_n

    xs = tuple(jnp.moveaxis(t, 1, 0) for t in (u, k_cum, qk, q_dec, k_dec, chunk_decay))
    state0 = jnp.zeros((bsz, heads, dk, dv), jnp.float32)
    _, o = lax.scan(step, state0, xs)
    return jnp.transpose(o, (1, 0, 3, 2, 4)).reshape(bsz, seq, heads, dv)


def gated_deltanet(qkv, beta_logit, alpha_logit, gate, conv_w, a_log, dt_bias, g_onorm):
    bsz, seq, _ = qkv.shape
    out_dtype = qkv.dtype
    f32 = jnp.float32
    qkv = jax.nn.silu(causal_depthwise_conv(qkv, conv_w)).astype(f32)
    q, k, v = jnp.split(qkv, 3, axis=-1)
    q = l2norm(q.reshape(bsz, seq, A_HEADS, A_DK)) * (A_DK ** -0.5)
    k = l2norm(k.reshape(bsz, seq, A_HEADS, A_DK))
    v = v.reshape(bsz, seq, A_HEADS, A_DV)
    beta = jax.nn.sigmoid(beta_logit.astype(f32))
    g = -jnp.exp(a_log.astype(f32)) * jax.nn.softplus(alpha_logit.astype(f32) + dt_bias.astype(f32))
    o = chunk_gated_delta_rule(q, k, v, g, beta)
    o = rmsnorm(o, g_onorm) * jax.nn.silu(gate.astype(f32).reshape(bsz, seq, A_HEADS, A_DV))
    return o.reshape(bsz, seq, A_HEADS * A_DV).astype(out_dtype)


def dilated_window_group(q, k, v, window, dilation, slopes):
    bsz, seq, heads, hd = q.shape
    f32 = jnp.float32
    n_back = window // dilation
    unit = B_BLOCK * dilation
    s_pad = -(-seq // unit) * unit
    nb = s_pad // unit
    pad = ((0, 0), (0, s_pad - seq), (0, 0), (0, 0))
    blocks = (bsz, nb, B_BLOCK, dilation, heads, hd)
    qb, kb, vb = [jnp.pad(t.astype(f32), pad).reshape(blocks) for t in (q, k, v)]
    prev = ((0, 0), (1, 0), (0, 0), (0, 0), (0, 0), (0, 0))
    kk = jnp.concatenate([jnp.pad(kb[:, :-1], prev), kb], axis=2)
    vv = jnp.concatenate([jnp.pad(vb[:, :-1], prev), vb], axis=2)
    s = jnp.einsum('bnirhd,bnjrhd->bnrhij', qb, kk) * (hd ** -0.5)
    i = jnp.arange(B_BLOCK)[:, None]
    j = jnp.arange(2 * B_BLOCK)[None, :]
    rel = i + B_BLOCK - j
    blk = jnp.arange(nb)[:, None, None]
    valid = (rel >= 0) & (rel <= n_back) & ((blk > 0) | (j >= B_BLOCK))
    bias = -slopes[:, None, None] * (dilation * rel).astype(f32)
    s = jnp.where(valid[None, :, None, None], s + bias, -jnp.inf)
    m = jnp.max(s, axis=-1, keepdims=True)
    p = jnp.exp(s - m)
    den = jnp.sum(p, axis=-1, keepdims=True)
    o = jnp.einsum('bnrhij,bnjrhd->bnirhd', p / den, vv)
    lse = jnp.transpose((m + jnp.log(den))[..., 0], (0, 1, 4, 2, 3))
    o = o.reshape(bsz, s_pad, heads, hd)[:, :seq]
    lse = lse.reshape(bsz, s_pad, heads)[:, :seq]
    return o, lse


def dilated_attention(qkv, gate):
    bsz, seq, _ = qkv.shape
    q, k, v = [t.reshape(bsz, seq, B_HEADS, B_HEAD_DIM) for t in jnp.split(qkv, 3, axis=-1)]
    slopes = alibi_slopes(B_HEADS)
    outs, lses = [], []
    for gi, (window, dilation) in enumerate(B_GROUPS):
        hs = slice(gi * B_HEADS_PER_GROUP, (gi + 1) * B_HEADS_PER_GROUP)
        o, lse = dilated_window_group(q[:, :, hs], k[:, :, hs], v[:, :, hs], window, dilation, slopes[hs])
        outs.append(o)
        lses.append(lse)
    weights = jax.nn.softmax(jnp.stack(lses, axis=0), axis=0)
    o = jnp.sum(weights[..., None] * jnp.stack(outs, axis=0), axis=0)
    o = o.reshape(bsz, seq, B_OUT_WIDTH) * jax.nn.silu(gate.astype(jnp.float32))
    return o.astype(qkv.dtype)


def spatial_gating(u, v, gate, ln_g, ln_b, w_s, b_s):
    bsz, seq, _ = u.shape
    n = seq // C_CHUNK
    u = jax.nn.gelu(u)
    v = layernorm(jax.nn.gelu(v), ln_g, ln_b)
    vc = v.reshape(bsz, n, C_CHUNK, C_GROUPS, C_GROUP_CH)
    causal = jnp.tril(jnp.ones((C_CHUNK, C_CHUNK), dtype=bool))
    w_causal = jnp.where(causal, w_s, 0.0).astype(v.dtype)
    sv = jnp.einsum('gts,bnsgc->bntgc', w_causal, vc) + jnp.transpose(b_s)[:, :, None]
    return u * sv.reshape(bsz, seq, C_WIDTH) * jax.nn.silu(gate)


def hybrid_layer(x, cond, w_ada, b_ada, g_pre, g_post, w_in, conv_a, a_log, dt_bias, g_onorm_a,
                 ln_c_g, ln_c_b, w_spatial, b_spatial, w_br_a, w_br_b, w_br_c, w_out):
    bsz, seq, _ = x.shape
    mod = cond @ w_ada + b_ada
    shift, scale, gate = jnp.split(mod, 3, axis=-1)
    h = rmsnorm(x, g_pre) * (1 + scale[:, None, :]) + shift[:, None, :]
    proj = h @ w_in
    offsets = np.cumsum(IN_SIZES)[:-1].tolist()
    (a_qkv, a_beta, a_alpha, a_gate, b_qkv, b_gate,
     c_u, c_v, c_gate, merge) = jnp.split(proj, offsets, axis=-1)
    y_a = gated_deltanet(a_qkv, a_beta, a_alpha, a_gate, conv_a, a_log, dt_bias, g_onorm_a) @ w_br_a
    y_b = dilated_attention(b_qkv, b_gate) @ w_br_b
    y_c = spatial_gating(c_u, c_v, c_gate, ln_c_g, ln_c_b, w_spatial, b_spatial) @ w_br_c
    gates = jax.nn.sigmoid(merge.astype(jnp.float32)).astype(x.dtype).reshape(bsz, seq, N_BRANCH, D_MODEL)
    y = gates[:, :, 0] * y_a + gates[:, :, 1] * y_b + gates[:, :, 2] * y_c
    out = y @ w_out
    return x + gate[:, None, :] * rmsnorm(out, g_post)


def setup_inputs(seed: int = 0) -> dict:
    key = jax.random.key(seed)
    ks = jax.random.split(key, 19)
    f32 = jnp.float32

    def nrm(k, shape, scale):
        return jax.random.normal(k, shape, f32) * scale

    def near_one(k, shape):
        return 1.0 + nrm(k, shape, 0.02)

    dt = jnp.exp(jax.random.uniform(ks[9], (DEPTH, A_HEADS), f32, math.log(0.001), math.log(0.1)))
    return {
        'x': nrm(ks[0], (BATCH, SEQ, D_MODEL), 1.0),
        'c': nrm(ks[1], (BATCH, D_MODEL), 1.0),
        'w_ada': nrm(ks[2], (DEPTH, D_MODEL, 3 * D_MODEL), 0.1 * D_MODEL ** -0.5),
        'b_ada': nrm(ks[3], (DEPTH, 3 * D_MODEL), 0.02),
        'g_pre': near_one(ks[4], (DEPTH, D_MODEL)),
        'g_post': near_one(ks[5], (DEPTH, D_MODEL)),
        'w_in': nrm(ks[6], (DEPTH, D_MODEL, IN_WIDTH), D_MODEL ** -0.5),
        'conv_a': nrm(ks[7], (DEPTH, A_CONV, 3 * A_WIDTH), A_CONV ** -0.5),
        'a_log': jnp.log(jax.random.uniform(ks[8], (DEPTH, A_HEADS), f32, 1.0, 16.0)),
        'dt_bias': dt + jnp.log(-jnp.expm1(-dt)),
        'g_onorm_a': near_one(ks[10], (DEPTH, A_DV)),
        'ln_c_g': near_one(ks[11], (DEPTH, C_WIDTH)),
        'ln_c_b': nrm(ks[12], (DEPTH, C_WIDTH), 0.02),
        'w_spatial': nrm(ks[13], (DEPTH, C_GROUPS, C_CHUNK, C_CHUNK), C_CHUNK ** -0.5),
        'b_spatial': near_one(ks[14], (DEPTH, C_GROUPS, C_CHUNK)),
        'w_br_a': nrm(ks[15], (DEPTH, A_HEADS * A_DV, D_MODEL), (A_HEADS * A_DV) ** -0.5),
        'w_br_b': nrm(ks[16], (DEPTH, B_OUT_WIDTH, D_MODEL), B_OUT_WIDTH ** -0.5),
        'w_br_c': nrm(ks[17], (DEPTH, C_WIDTH, D_MODEL), C_WIDTH ** -0.5),
        'w_out': nrm(ks[18], (DEPTH, D_MODEL, D_MODEL), D_MODEL ** -0.5),
    }


def reference(x, c, w_ada, b_ada, g_pre, g_post, w_in, conv_a, a_log, dt_bias, g_onorm_a,
              ln_c_g, ln_c_b, w_spatial, b_spatial, w_br_a, w_br_b, w_br_c, w_out):
    cond = jax.nn.silu(c)
    for l in range(DEPTH):
        x = hybrid_layer(x, cond, w_ada[l], b_ada[l], g_pre[l], g_post[l], w_in[l], conv_a[l],
                         a_log[l], dt_bias[l], g_onorm_a[l], ln_c_g[l], ln_c_b[l], w_spatial[l],
                         b_spatial[l], w_br_a[l], w_br_b[l], w_br_c[l], w_out[l])
    return x
```

```python
import numpy as np
import concourse.bass as bass
import concourse.mybir as mybir
from concourse.bass_utils import run_bass_kernel_spmd

F32 = mybir.dt.float32
BF16 = mybir.dt.bfloat16
AF = mybir.ActivationFunctionType
ALU = mybir.AluOpType
AX = mybir.AxisListType

D = 4096
KC = 32
NW = 36896
A_QKV, A_BETA, A_ALPHA, A_GATE = 0, 6144, 6160, 6176
B_QKV, B_GATE = 8224, 17440
C_U, C_V, C_GATE = 18464, 20512, 22560
MERGE = 24608
EPS = 1e-6
NEG = -30000.0


class Buf:
    def __init__(self, ap=None):
        self.ap = ap
        self.w = {}
        self.r = {}


class Prog:
    ENG = ("pe", "act", "dve", "pool", "sp")

    def __init__(self, nc, stack):
        self.nc = nc
        self.q = {e: [] for e in self.ENG}
        self.cnt = {e: 0 for e in self.ENG}
        self.seen = {e: {} for e in self.ENG}
        self.esem = {e: stack.enter_context(nc.semaphore("s_" + e)) for e in self.ENG if e != "sp"}
        self.dsem = {}
        self.dcnt = {}
        self.dnext = {}
        for qn, k in (("sp", 8), ("pool", 4), ("act", 4)):
            self.dsem[qn] = [stack.enter_context(nc.semaphore("d_%s%d" % (qn, i))) for i in range(k)]
            self.dcnt[qn] = [0] * k
            self.dnext[qn] = 0
        self.ninst = 0

    def _waits(self, eng, deps):
        for key, val in deps.items():
            if self.seen[eng].get(key, 0) < val:
                self.seen[eng][key] = val
                sem = key[1]
                self.q[eng].append(lambda e, sem=sem, val=val: e.wait_ge(sem, val))

    def _deps(self, eng, r, w):
        deps = {}
        for b in r:
            for k, v in b.w.items():
                deps[k] = max(deps.get(k, 0), v)
            if getattr(b, "psum", False):
                for k, v in b.r.items():
                    if k[0] != "e_" + eng:
                        deps[k] = max(deps.get(k, 0), v)
        for b in w:
            for dct in (b.w, b.r):
                for k, v in dct.items():
                    deps[k] = max(deps.get(k, 0), v)
        if eng == "pe":
            deps = {k: v for k, v in deps.items() if k[0] != "e_pe"}
        return deps

    def _mark(self, tok, r, w):
        k, v = tok
        for b in r:
            b.r[k] = max(b.r.get(k, 0), v)
        for b in w:
            b.w[k] = max(b.w.get(k, 0), v)

    def op(self, eng, fn, r=(), w=(), sig=True):
        self.ninst += 1
        self._waits(eng, self._deps(eng, r, w))
        sem = self.esem[eng]
        if sig:
            self.cnt[eng] += 1
            self.q[eng].append(lambda e, fn=fn, sem=sem: fn(e).then_inc(sem, 1))
            tok = (("e_" + eng, sem), self.cnt[eng])
        else:
            self.q[eng].append(lambda e, fn=fn: fn(e))
            tok = (("e_" + eng, sem), self.cnt[eng] + 1)
        self._mark(tok, r, w)

    def dma(self, out, in_, r=(), w=(), q="sp", **kw):
        self.ninst += 1
        i = self.dnext[q]
        self.dnext[q] = (i + 1) % len(self.dsem[q])
        sem = self.dsem[q][i]
        key = ("d_%s%d" % (q, i), sem)
        deps = self._deps(q, r, w)
        if self.dcnt[q][i] > 0:
            deps[key] = max(deps.get(key, 0), 16 * self.dcnt[q][i])
        self._waits(q, deps)
        self.dcnt[q][i] += 1
        self.q[q].append(lambda e, out=out, in_=in_, sem=sem, kw=kw: e.dma_start(out=out, in_=in_, **kw).then_inc(sem, 16))
        self._mark((key, 16 * self.dcnt[q][i]), r, w)

    def barrier(self):
        deps = {}
        for e, s in self.esem.items():
            if self.cnt[e] > 0:
                deps[("e_" + e, s)] = self.cnt[e]
        for qn in self.dsem:
            for i, s in enumerate(self.dsem[qn]):
                if self.dcnt[qn][i] > 0:
                    deps[("d_%s%d" % (qn, i), s)] = 16 * self.dcnt[qn][i]
        for e in self.ENG:
            self._waits(e, dict(deps))

    def clear(self):
        self.q = {e: [] for e in self.ENG}

    def finish(self, bufs):
        deps = {}
        for b in bufs:
            for k, v in b.w.items():
                deps[k] = max(deps.get(k, 0), v)
        self._waits("sp", deps)

    def emit(self, block):
        nc = self.nc

        @block.tensor
        def _(e):
            for f in self.q["pe"]:
                f(e)

        @block.scalar
        def _(e):
            for f in self.q["act"]:
                f(e)

        @block.vector
        def _(e):
            for f in self.q["dve"]:
                f(e)

        @block.gpsimd
        def _(e):
            for f in self.q["pool"]:
                f(e)

        @block.sync
        def _(e):
            for f in self.q["sp"]:
                f(e)


def build(T, L, dbg=None):
    from contextlib import ExitStack
    nc = bass.Bass("TRN2", target_bir_lowering=False)
    NT = T // 128
    stack = ExitStack()
    uid = [0]

    def din(name, shape, dt=F32):
        return nc.dram_tensor(name, list(shape), dt, kind="ExternalInput").ap()

    def dscr(name, shape, dt=F32, kind=None):
        if kind is None:
            return Buf(nc.dram_tensor(name, list(shape), dt).ap())
        return Buf(nc.dram_tensor(name, list(shape), dt, kind=kind).ap())

    class Lazy(dict):
        def __init__(self, shapes):
            dict.__init__(self)
            self.shapes = shapes

        def __missing__(self, k):
            nm, shp = self.shapes[k]
            self[k] = din(nm, shp)
            return self[k]

    x_in = din("x", [T, D]) if (dbg is None or "1" in dbg or "f" in dbg) else None
    cT_in = din("cT", [128, KC])
    out_ap = nc.dram_tensor("out", [T, D], F32, kind="ExternalOutput").ap()
    ident_in = din("ident", [128, 128])
    ltri_in = din("ltri", [128, 128])
    lstr_in = din("lstr", [128, 128])
    sel_in = din("sel", [16, 16 * 128])
    bmask_in = din("bmask", [24, 128, 256])
    W = []
    for l in range(L):
        W.append(Lazy(dict(
            w_ada=("w_ada%d" % l, [D, 3 * D]), b_adaT=("b_adaT%d" % l, [128, 64]),
            b_gateB=("b_gateB%d" % l, [128, D]),
            g_preT=("g_preT%d" % l, [128, KC]), g_postB=("g_postB%d" % l, [128, D]),
            w_in=("w_in%d" % l, [D, NW]), convT=("convT%d" % l, [128, 48, 4]),
            a_logB=("a_logB%d" % l, [128, 16]), dt_biasB=("dt_biasB%d" % l, [128, 16]),
            g_onormB=("g_onormB%d" % l, [128, 128]),
            ln_gB=("ln_gB%d" % l, [128, 2048]), ln_bB=("ln_bB%d" % l, [128, 2048]),
            w_spT=("w_spT%d" % l, [128, 16, 128]), b_spT=("b_spT%d" % l, [128, 16]),
            w_br=("w_br%d" % l, [5120, D]), w_out=("w_out%d" % l, [D, D]),
        )))
    hT_d = dscr("hT_d", [NT, 128, KC, 128], BF16)
    REG = [(0, 6144), (6144, 8224), (8224, 11296), (11296, 14368), (14368, 17440), (17440, 18464),
           (18464, 24608), (24608, 28704), (28704, 32800), (32800, 36896)]
    p_in = "ExternalInput" if (dbg is not None and "g" not in dbg) else None
    P_regs = [[a, b, None, i] for i, (a, b) in enumerate(REG)]

    def preg(e):
        if e[2] is None:
            e[2] = dscr("P_d%d" % e[3], [T, e[1] - e[0]], kind=p_in)
        return e[2]

    def pc(rows, c0, c1):
        for e in P_regs:
            if e[0] <= c0 and c1 <= e[1]:
                buf = preg(e)
                return buf.ap[rows, c0 - e[0]:c1 - e[0]], buf
        raise AssertionError((c0, c1))

    def p_store(t, n0, nw, s):
        rows = slice(t * 128, (t + 1) * 128)
        for e in P_regs:
            a, b = e[0], e[1]
            lo = max(a, n0); hi = min(b, n0 + nw)
            if lo < hi:
                buf = preg(e)
                P.dma(buf.ap[rows, lo - a:hi - a], s.ap[:, lo - n0:hi - n0], r=[s], w=[buf])
    o_kind = None
    if dbg is not None:
        o_kind = "ExternalOutput" if any(c in dbg for c in "abc") else "ExternalInput"
    o_d = dscr("o_d", [T, 5120], kind=o_kind)
    og_d = dscr("og_d", [3, T, 8, 132], kind=("ExternalOutput" if (dbg is not None and "b" in dbg) else None))
    oT_d = dscr("oT_d", [NT, 128, 40, 128], BF16)
    yT_d = dscr("yT_d", [NT, 128, KC, 128], BF16)
    ob_d = dscr("ob_d", [T, D])
    x1_d = dscr("x1_d", [T, D])
    xin_b = Buf(x_in)
    if dbg is not None and "f" not in dbg:
        pass
    out_b = Buf(out_ap)

    P = Prog(nc, stack)

    def gsb(name, shape, dt=F32):
        return Buf(stack.enter_context(nc.sbuf_tensor("g_" + name, list(shape), dt)))

    psb = [Buf(stack.enter_context(nc.psum_tensor("ps%d" % i, [128, 512], F32))) for i in range(8)]
    for b_ in psb:
        b_.psum = True
    pcount = [0]

    def psum():
        pcount[0] += 1
        return psb[pcount[0] % 8]

    import os as _os
    kstop = int(_os.environ.get("KSTOP", "99"))
    nstage = [0]

    def run_stage(fn, letter="k"):
        nstage[0] += 1
        if nstage[0] > kstop:
            return
        if dbg is not None and letter != "k" and letter not in dbg:
            return
        with ExitStack() as st:
            def sb(name, shape, dt=F32):
                uid[0] += 1
                return Buf(st.enter_context(nc.sbuf_tensor("t_%s_%d" % (name, uid[0]), list(shape), dt)))
            fn(sb)
            P.barrier()
            with nc.Block() as block:
                P.emit(block)
            P.clear()

    ident = gsb("ident", [128, 128]); ltri = gsb("ltri", [128, 128]); lstr = gsb("lstr", [128, 128])
    sel = gsb("sel", [16, 2048]); ones = gsb("ones", [128, 128])
    modT = gsb("modT", [128, 64]); gmod = gsb("gmod", [128, KC]); ggB = gsb("ggB", [128, D])
    condT = gsb("condT", [128, KC])
    ss = gsb("ss", [128, 4]); rs = gsb("rs", [128, 4])

    def rsqrt_to(dst, src, scale, n):
        P.op("dve", lambda e: e.tensor_scalar(out=dst.ap[:, 0:n], in0=src.ap[:, 0:n], scalar1=scale, scalar2=EPS,
                                              op0=ALU.mult, op1=ALU.add), r=[src], w=[dst])
        P.op("act", lambda e: e.activation(out=dst.ap[:, 0:n], in_=dst.ap[:, 0:n], func=AF.Sqrt), r=[dst], w=[dst])
        P.op("dve", lambda e: e.reciprocal(out=dst.ap[:, 0:n], in_=dst.ap[:, 0:n]), r=[dst], w=[dst])

    def mm(out, lhsT, rhs, r, w, start=True, stop=True, sig=None):
        P.op("pe", lambda e: e.matmul(out, lhsT=lhsT, rhs=rhs, start=start, stop=stop), r=r, w=w,
             sig=stop if sig is None else sig)

    def tr(out, in_, r, w):
        P.op("pe", lambda e: e.transpose(out, in_, ident.ap[:]), r=list(r) + [ident], w=w)

    def V(eng, fname, r, w, **kw):
        P.op(eng, lambda e: getattr(e, fname)(**kw), r=r, w=w)

    def s_const(sb):
        cT = sb("cT", [128, KC])
        P.dma(ident.ap[:], ident_in[:, :], w=[ident])
        P.dma(ltri.ap[:], ltri_in[:, :], w=[ltri])
        P.dma(lstr.ap[:], lstr_in[:, :], w=[lstr])
        P.dma(sel.ap[:], sel_in[:, :], w=[sel])
        V("pool", "memset", [], [ones], ap=ones.ap[:], constant=1.0)
        P.dma(cT.ap[:], cT_in[:, :], w=[cT])
        V("act", "activation", [cT], [condT], out=condT.ap[:], in_=cT.ap[:], func=AF.Silu)
    run_stage(s_const)

    cur_x = xin_b
    for l in range(L):
        Wl = W[l]
        nxt_x = out_b if l == L - 1 else x1_d
        xv = cur_x.ap.rearrange("(n p) d -> n p d", p=128) if cur_x.ap is not None else None

        def s0(sb):
            wa = [sb("wa%d" % i, [128, KC, 128]) for i in range(2)]
            condB = sb("condB", [128, KC, 128])
            b_adaT = sb("b_adaT", [128, 64]); g_preT = sb("g_preT", [128, KC]); gpost = sb("gpost", [128, D])
            for k in range(KC):
                V("dve", "tensor_scalar", [ones, condT], [condB], out=condB.ap[:, k, :], in0=ones.ap[:], scalar1=condT.ap[:, k:k + 1],
                  scalar2=None, op0=ALU.mult)
            P.dma(b_adaT.ap[:], Wl["b_adaT"][:, :], w=[b_adaT])
            P.dma(g_preT.ap[:], Wl["g_preT"][:, :], w=[g_preT])
            P.dma(ggB.ap[:], Wl["b_gateB"][:, :], w=[ggB])
            P.dma(gpost.ap[:], Wl["g_postB"][:, :], w=[gpost])
            wv = Wl["w_ada"].rearrange("(c p) n -> p c n", p=128)
            mps = psum()
            for j in range(96):
                wt = wa[j % 2]
                P.dma(wt.ap[:], wv[:, :, j * 128:(j + 1) * 128], w=[wt])
                if j < 64:
                    for k in range(KC):
                        mm(mps.ap[:, j:j + 1], wt.ap[:, k, :], condT.ap[:, k:k + 1], [wt, condT], [mps], start=(k == 0), stop=(k == KC - 1))
                else:
                    gp = psum()
                    if gp is mps:
                        gp = psum()
                    for k in range(KC):
                        mm(gp.ap[:, 0:128], condB.ap[:, k, :], wt.ap[:, k, :], [wt, condB], [gp], start=(k == 0), stop=(k == KC - 1))
                    c0 = (j - 64) * 128
                    V("dve", "tensor_tensor", [gp, ggB], [ggB], out=ggB.ap[:, c0:c0 + 128], in0=gp.ap[:, 0:128], in1=ggB.ap[:, c0:c0 + 128], op=ALU.add)
            V("dve", "tensor_tensor", [mps, b_adaT], [modT], out=modT.ap[:], in0=mps.ap[:, 0:64], in1=b_adaT.ap[:], op=ALU.add)
            V("dve", "scalar_tensor_tensor", [modT, g_preT], [gmod], out=gmod.ap[:], in0=modT.ap[:, 32:64], scalar=1.0, in1=g_preT.ap[:],
              op0=ALU.add, op1=ALU.mult)
            V("dve", "tensor_tensor", [ggB, gpost], [ggB], out=ggB.ap[:], in0=ggB.ap[:], in1=gpost.ap[:], op=ALU.mult)
        run_stage(s0, "0")

        def s1(sb):
            xts = [sb("xt%d" % i, [128, D]) for i in range(2)]
            junk = sb("junk", [128, D])
            hss = [sb("hs%d" % i, [128, KC, 128], BF16) for i in range(2)]
            for t in range(NT):
                xt = xts[t % 2]; hs = hss[t % 2]
                P.dma(xt.ap[:], xv[t], r=[cur_x], w=[xt])
                V("act", "activation", [xt], [junk, ss], out=junk.ap[:], in_=xt.ap[:], func=AF.Square, accum_out=ss.ap[:, 0:1])
                rsqrt_to(rs, ss, 1.0 / D, 1)
                V("dve", "tensor_scalar", [xt, rs], [xt], out=xt.ap[:], in0=xt.ap[:], scalar1=rs.ap[:, 0:1], scalar2=None, op0=ALU.mult)
                for c4 in range(8):
                    pt = psum()
                    for i in range(4):
                        c = c4 * 4 + i
                        tr(pt.ap[:, i * 128:(i + 1) * 128], xt.ap[:, c * 128:(c + 1) * 128], [xt], [pt])
                    for i in range(4):
                        c = c4 * 4 + i
                        if i % 2 == 0:
                            V("act", "activation", [pt, gmod, modT], [hs], out=hs.ap[:, c, :], in_=pt.ap[:, i * 128:(i + 1) * 128], func=AF.Identity,
                              scale=gmod.ap[:, c:c + 1], bias=modT.ap[:, c:c + 1])
                        else:
                            V("dve", "tensor_scalar", [pt, gmod, modT], [hs], out=hs.ap[:, c, :], in0=pt.ap[:, i * 128:(i + 1) * 128],
                              scalar1=gmod.ap[:, c:c + 1], scalar2=modT.ap[:, c:c + 1], op0=ALU.mult, op1=ALU.add)
                P.dma(hT_d.ap[t], hs.ap[:], r=[hs], w=[hT_d])
        run_stage(s1, "1")

        def gemm_stage(a_d, kc, wd, n_total, dst, letter):
            def body(sb):
                wts = [sb("wt%d" % i, [128, kc, 512], BF16) for i in range(2)]
                ats = [sb("at%d" % i, [128, kc, 128], BF16) for i in range(4)]
                stg = [sb("stg%d" % i, [128, 512]) for i in range(4)]
                wvv = wd.rearrange("(c p) n -> p c n", p=128)
                nblk = (n_total + 511) // 512
                ac = 0; ec = 0
                for nb in range(nblk):
                    n0 = nb * 512
                    nw = min(512, n_total - n0)
                    wt = wts[nb % 2]
                    for k4 in range(0, kc, 8):
                        P.dma(wt.ap[:, k4:k4 + 8, 0:nw], wvv[:, k4:k4 + 8, n0:n0 + nw], w=[wt], q="pool")
                    for t in range(NT):
                        at = ats[ac % 4]; ac += 1
                        P.dma(at.ap[:], a_d.ap[t], r=[a_d], w=[at])
                        ps = psum()
                        for k in range(kc):
                            mm(ps.ap[:, 0:nw], at.ap[:, k, :], wt.ap[:, k, 0:nw], [at, wt], [ps], start=(k == 0), stop=(k == kc - 1))
                        s = stg[ec % 4]
                        if ec % 2 == 0:
                            V("act", "activation", [ps], [s], out=s.ap[:, 0:nw], in_=ps.ap[:, 0:nw], func=AF.Identity)
                        else:
                            V("dve", "tensor_copy", [ps], [s], out=s.ap[:, 0:nw], in_=ps.ap[:, 0:nw])
                        ec += 1
                        if dst is None:
                            p_store(t, n0, nw, s)
                        else:
                            P.dma(dst.ap[t * 128:(t + 1) * 128, n0:n0 + nw], s.ap[:, 0:nw], r=[s], w=[dst])
            run_stage(body, letter)

        if dbg is None or "g" in dbg:
            gemm_stage(hT_d, KC, Wl["w_in"], NW, None, "g")

        def s3c(sb):
            lng = sb("lng", [128, 2048]); lnb = sb("lnb", [128, 2048])
            wsp = sb("wsp", [128, 16, 128]); wspb = sb("wspb", [128, 16, 128], BF16); bsp = sb("bsp", [128, 16])
            P.dma(lng.ap[:], Wl["ln_gB"][:, :], w=[lng]); P.dma(lnb.ap[:], Wl["ln_bB"][:, :], w=[lnb])
            P.dma(wsp.ap[:], Wl["w_spT"][:, :, :], w=[wsp]); P.dma(bsp.ap[:], Wl["b_spT"][:, :], w=[bsp])
            for g in range(16):
                V("dve", "tensor_tensor", [wsp, ltri], [wspb], out=wspb.ap[:, g, :], in0=wsp.ap[:, g, :], in1=ltri.ap[:], op=ALU.mult)
            uvg = [sb("uvg%d" % i, [128, 3, 2048]) for i in range(2)]
            tmp = sb("tmpc", [128, 2, 2048]); vnb = sb("vnb", [128, 2048], BF16)
            st6 = sb("st6", [128, 4, 6]); mv = sb("mv", [128, 4]); orow = [sb("orow%d" % i, [128, 2048]) for i in range(2)]

            def gelu(xa, ta, eng2):
                V("pool", "tensor_tensor", [uv], [tmp], out=ta, in0=xa, in1=xa, op=ALU.mult)
                V(eng2, "tensor_scalar", [tmp], [tmp], out=ta, in0=ta, scalar1=0.044715, scalar2=1.0, op0=ALU.mult, op1=ALU.add)
                V("pool", "tensor_tensor", [uv, tmp], [tmp], out=ta, in0=ta, in1=xa, op=ALU.mult)
                V("act", "activation", [tmp], [tmp], out=ta, in_=ta, func=AF.Sigmoid, scale=1.5957691216)
                V(eng2, "tensor_tensor", [uv, tmp], [uv], out=xa, in0=xa, in1=ta, op=ALU.mult)
            for t in range(NT):
                uv = uvg[t % 2]; orw = orow[t % 2]
                rows = slice(t * 128, (t + 1) * 128)
                pa_, pb_ = pc(rows, C_U, C_U + 6144)
                P.dma(uv.ap[:], pa_.rearrange("p (a b) -> p a b", a=3), r=[pb_], w=[uv])
                gelu(uv.ap[:, 0, :], tmp.ap[:, 0, :], "dve")
                gelu(uv.ap[:, 1, :], tmp.ap[:, 1, :], "dve")
                for i in range(4):
                    V("dve", "bn_stats", [uv], [st6], out=st6.ap[:, i, :], in_=uv.ap[:, 1, i * 512:(i + 1) * 512])
                V("dve", "bn_aggr", [st6], [mv], out=mv.ap[:, 0:2], in_=st6.ap[:].rearrange("p a b -> p (a b)"))
                rsqrt_to(rs, Buf_view(mv, 1), 1.0, 1)
                V("dve", "tensor_scalar", [uv, mv, rs], [uv], out=uv.ap[:, 1, :], in0=uv.ap[:, 1, :], scalar1=mv.ap[:, 0:1], scalar2=rs.ap[:, 0:1],
                  op0=ALU.subtract, op1=ALU.mult)
                V("pool", "tensor_tensor", [uv, lng], [uv], out=uv.ap[:, 1, :], in0=uv.ap[:, 1, :], in1=lng.ap[:], op=ALU.mult)
                V("dve", "tensor_tensor", [uv, lnb], [vnb], out=vnb.ap[:], in0=uv.ap[:, 1, :], in1=lnb.ap[:], op=ALU.add)
                V("act", "activation", [uv], [uv], out=uv.ap[:, 2, :], in_=uv.ap[:, 2, :], func=AF.Silu)
                V("pool", "tensor_tensor", [uv], [uv], out=uv.ap[:, 0, :], in0=uv.ap[:, 0, :], in1=uv.ap[:, 2, :], op=ALU.mult)
                for g4 in range(4):
                    ps = psum()
                    for i in range(4):
                        g = g4 * 4 + i
                        mm(ps.ap[:, i * 128:(i + 1) * 128], wspb.ap[:, g, :], vnb.ap[:, g * 128:(g + 1) * 128], [wspb, vnb], [ps])
                    for i in range(4):
                        g = g4 * 4 + i
                        V("dve", "scalar_tensor_tensor", [ps, bsp, uv], [orw], out=orw.ap[:, g * 128:(g + 1) * 128], in0=ps.ap[:, i * 128:(i + 1) * 128],
                          scalar=bsp.ap[:, g:g + 1], in1=uv.ap[:, 0, g * 128:(g + 1) * 128], op0=ALU.add, op1=ALU.mult)
                P.dma(o_d.ap[rows, 3072:5120], orw.ap[:], r=[orw], w=[o_d])

        def Buf_view(b, col):
            v = Buf(b.ap[:, col:col + 1]); v.w = b.w; v.r = b.r
            return v
        run_stage(s3c, "c")

        def s3b(sb):
            bm = sb("bm", [128, 24, 256])
            P.dma(bm.ap[:], bmask_in.rearrange("h p j -> p h j"), w=[bm])
            qkv = [sb("qkv%d" % i, [128, 3, 128]) for i in range(3)]
            kT2 = [sb("kT2_%d" % i, [128, 128], BF16) for i in range(3)]
            vbf = [sb("vbf%d" % i, [128, 128], BF16) for i in range(3)]
            qT = [sb("qTb%d" % i, [128, 128], BF16) for i in range(2)]
            sbt = [sb("sbt%d" % i, [128, 256]) for i in range(2)]
            pb = [sb("pb%d" % i, [128, 256]) for i in range(2)]
            pT = [sb("pT%d" % i, [128, 256], BF16) for i in range(2)]
            st = [sb("stb%d" % i, [128, 8]) for i in range(2)]
            oo = [sb("oo%d" % i, [128, 132]) for i in range(3)]
            it = 0
            kb = int(_os.environ.get("KB", "0")); kbi = int(_os.environ.get("KBI", "99")); kbn = int(_os.environ.get("KBN", "999999"))
            for gi, d in enumerate((1, 4, 16)):
                nblk = T // (128 * d) if kb != 1 else 0
                for j in range(8):
                    hidx = gi * 8 + j
                    cq = B_QKV + hidx * 128
                    for r in range(d):
                        for n in range(nblk):
                            if it >= kbn:
                                continue
                            a = qkv[it % 3]; kt = kT2[it % 3]; vb = vbf[it % 3]
                            kp = kT2[(it - 1) % 3]; vp = vbf[(it - 1) % 3]
                            q_ = qT[it % 2]; s_ = sbt[it % 2]; p_ = pb[it % 2]; pt_ = pT[it % 2]; t_ = st[it % 2]; o_ = oo[it % 3]
                            it += 1
                            t0 = n * 128 * d + r
                            for s3 in range(3):
                                pa_, pb_ = pc(slice(t0, t0 + 127 * d + 1, d), cq + s3 * 3072, cq + s3 * 3072 + 128)
                                P.dma(a.ap[:, s3, :], pa_, r=[pb_], w=[a])
                            if kbi < 1: continue
                            ps = psum()
                            kbx = int(_os.environ.get("KBX", "9"))
                            ps2 = psum() if kbx == 10 else ps
                            tr(ps.ap[:, 0:128], a.ap[:, 0, :], [a], [ps])
                            tr(ps2.ap[:, 128:256], a.ap[:, 1, :], [a], [ps2])
                            if kbx < 2: continue
                            if kbx == 8:
                                V("dve", "tensor_scalar", [ps], [s_], out=s_.ap[:, 0:128], in0=ps.ap[:, 128:256], scalar1=1.0, scalar2=None, op0=ALU.mult)
                                continue
                            V("act", "activation", [ps], [q_], out=q_.ap[:], in_=ps.ap[:, 0:128], func=AF.Identity)
                            if kbx < 3: continue
                            if kbx == 4:
                                V("dve", "tensor_scalar", [ps], [s_], out=s_.ap[:, 0:128], in0=ps.ap[:, 128:256], scalar1=1.0, scalar2=None, op0=ALU.mult)
                                continue
                            if kbx == 5:
                                V("dve", "tensor_scalar", [a], [kt], out=kt.ap[:], in0=a.ap[:, 1, :], scalar1=1.0, scalar2=None, op0=ALU.mult)
                                continue
                            if kbx == 6:
                                V("dve", "tensor_scalar", [a], [s_], out=s_.ap[:, 0:128], in0=a.ap[:, 1, :], scalar1=1.0, scalar2=None, op0=ALU.mult)
                                continue
                            V("dve", "tensor_scalar", [ps2], [kt], out=kt.ap[:], in0=ps2.ap[:, 128:256], scalar1=1.0, scalar2=None, op0=ALU.mult)
                            if kbi < 2: continue
                            V("pool", "tensor_copy", [a], [vb], out=vb.ap[:], in_=a.ap[:, 2, :])
                            if kbi < 3: continue
                            sp_ = psum()
                            k0 = 0 if n > 0 else 128
                            if n > 0:
                                mm(sp_.ap[:, 0:128], q_.ap[:], kp.ap[:], [q_, kp], [sp_])
                            mm(sp_.ap[:, 128:256], q_.ap[:], kt.ap[:], [q_, kt], [sp_])
                            if kbi < 4: continue
                            V("dve", "scalar_tensor_tensor", [sp_, bm], [s_], out=s_.ap[:, k0:256], in0=sp_.ap[:, k0:256], scalar=128.0 ** -0.5,
                              in1=bm.ap[:, hidx, k0:256], op0=ALU.mult, op1=ALU.add)
                            if kbi < 5: continue
                            V("dve", "reduce_max", [s_], [t_], out=t_.ap[:, 0:1], in_=s_.ap[:, k0:256], axis=AX.X)
                            V("dve", "tensor_scalar", [t_], [t_], out=t_.ap[:, 1:2], in0=t_.ap[:, 0:1], scalar1=-1.0, scalar2=None, op0=ALU.mult)
                            if kbi < 6: continue
                            V("act", "activation", [s_, t_], [p_, t_], out=p_.ap[:, k0:256], in_=s_.ap[:, k0:256], func=AF.Exp, bias=t_.ap[:, 1:2],
                              accum_out=t_.ap[:, 2:3])
                            if kbi < 7: continue
                            tp = psum()
                            if n > 0:
                                tr(tp.ap[:, 0:128], p_.ap[:, 0:128], [p_], [tp])
                            tr(tp.ap[:, 128:256], p_.ap[:, 128:256], [p_], [tp])
                            V("act", "activation", [tp], [pt_], out=pt_.ap[:, k0:256], in_=tp.ap[:, k0:256], func=AF.Identity)
                            if kbi < 8: continue
                            op_ = psum()
                            if n > 0:
                                mm(op_.ap[:, 0:128], pt_.ap[:, 0:128], vp.ap[:], [pt_, vp], [op_], start=True, stop=False)
                            mm(op_.ap[:, 0:128], pt_.ap[:, 128:256], vb.ap[:], [pt_, vb], [op_], start=(n == 0), stop=True)
                            if kbi < 9: continue
                            V("dve", "reciprocal", [t_], [t_], out=t_.ap[:, 3:4], in_=t_.ap[:, 2:3])
                            V("dve", "tensor_scalar", [op_, t_], [o_], out=o_.ap[:, 0:128], in0=op_.ap[:, 0:128], scalar1=t_.ap[:, 3:4], scalar2=None, op0=ALU.mult)
                            if kbi < 10: continue
                            V("act", "activation", [t_], [t_], out=t_.ap[:, 4:5], in_=t_.ap[:, 2:3], func=AF.Ln)
                            V("dve", "tensor_tensor", [t_], [o_], out=o_.ap[:, 128:129], in0=t_.ap[:, 4:5], in1=t_.ap[:, 0:1], op=ALU.add)
                            if kbi < 11: continue
                            P.dma(og_d.ap[gi, t0:t0 + 127 * d + 1:d, j, :], o_.ap[:], r=[o_], w=[og_d])
            ogt = [sb("ogt%d" % i, [128, 3, 8, 132]) for i in range(2)]
            gt = [sb("gtb%d" % i, [128, 1024]) for i in range(2)]
            wgt = sb("wgt", [128, 3, 8]); wmx = sb("wmx", [128, 8]); acc = [sb("accb%d" % i, [128, 1024]) for i in range(2)]
            for t in range(NT if kb != 2 else 0):
                rows = slice(t * 128, (t + 1) * 128)
                og = ogt[t % 2]; g_ = gt[t % 2]; ac = acc[t % 2]
                for gi in range(3):
                    P.dma(og.ap[:, gi], og_d.ap[gi, rows], r=[og_d], w=[og])
                pa_, pb_ = pc(rows, B_GATE, B_GATE + 1024)
                P.dma(g_.ap[:], pa_, r=[pb_], w=[g_])
                V("dve", "tensor_tensor", [og], [wmx], out=wmx.ap[:], in0=og.ap[:, 0, :, 128], in1=og.ap[:, 1, :, 128], op=ALU.max)
                V("dve", "tensor_tensor", [og, wmx], [wmx], out=wmx.ap[:], in0=wmx.ap[:], in1=og.ap[:, 2, :, 128], op=ALU.max)
                for gi in range(3):
                    V("dve", "tensor_tensor", [og, wmx], [wgt], out=wgt.ap[:, gi, :], in0=og.ap[:, gi, :, 128], in1=wmx.ap[:], op=ALU.subtract)
                V("act", "activation", [wgt], [wgt], out=wgt.ap[:], in_=wgt.ap[:], func=AF.Exp)
                V("dve", "tensor_tensor", [wgt], [wmx], out=wmx.ap[:], in0=wgt.ap[:, 0, :], in1=wgt.ap[:, 1, :], op=ALU.add)
                V("dve", "tensor_tensor", [wgt, wmx], [wmx], out=wmx.ap[:], in0=wmx.ap[:], in1=wgt.ap[:, 2, :], op=ALU.add)
                V("dve", "reciprocal", [wmx], [wmx], out=wmx.ap[:], in_=wmx.ap[:])
                for gi in range(3):
                    V("dve", "tensor_tensor", [wgt, wmx], [wgt], out=wgt.ap[:, gi, :], in0=wgt.ap[:, gi, :], in1=wmx.ap[:], op=ALU.mult)
                V("act", "activation", [g_], [g_], out=g_.ap[:], in_=g_.ap[:], func=AF.Silu)
                for j in range(8):
                    cs = slice(j * 128, (j + 1) * 128)
                    V("dve", "tensor_scalar", [og, wgt], [ac], out=ac.ap[:, cs], in0=og.ap[:, 0, j, 0:128], scalar1=wgt.ap[:, 0, j:j + 1], scalar2=None, op0=ALU.mult)
                    for gi in (1, 2):
                        V("dve", "scalar_tensor_tensor", [og, wgt, ac], [ac], out=ac.ap[:, cs], in0=og.ap[:, gi, j, 0:128], scalar=wgt.ap[:, gi, j:j + 1],
                          in1=ac.ap[:, cs], op0=ALU.mult, op1=ALU.add)
                V("pool", "tensor_tensor", [ac, g_], [ac], out=ac.ap[:], in0=ac.ap[:], in1=g_.ap[:], op=ALU.mult)
                P.dma(o_d.ap[rows, 2048:3072], ac.ap[:], r=[ac], w=[o_d])
        run_stage(s3b, "b")

        def s3a(sb):
            cw = sb("cw", [128, 48, 4]); nea = sb("nea", [128, 16]); dtb = sb("dtb", [128, 16]); gon = sb("gon", [128, 128])
            P.dma(cw.ap[:], Wl["convT"][:, :, :], w=[cw]); P.dma(nea.ap[:], Wl["a_logB"][:, :], w=[nea])
            P.dma(dtb.ap[:], Wl["dt_biasB"][:, :], w=[dtb]); P.dma(gon.ap[:], Wl["g_onormB"][:, :], w=[gon])
            V("act", "activation", [nea], [nea], out=nea.ap[:], in_=nea.ap[:], func=AF.Exp)
            V("dve", "tensor_scalar", [nea], [nea], out=nea.ap[:], in0=nea.ap[:], scalar1=-1.0, scalar2=None, op0=ALU.mult)
            S = sb("S", [128, 16, 128]); halo = sb("halo", [128, 16, 3, 3])
            V("pool", "memset", [], [S], ap=S.ap[:], constant=0.0)
            V("pool", "memset", [], [halo], ap=halo.ap[:], constant=0.0)
            ba = sb("ba", [128, 32]); tk = sb("tk", [128, 8, 16])
            rowsT = sb("rowsT", [16, 256])
            qkvg = [sb("qkvg%d" % i, [128, 4, 128]) for i in range(2)]
            xc = [sb("xc%d" % i, [128, 3, 131]) for i in range(2)]
            accs = [sb("acc%d" % i, [128, 3, 128]) for i in range(2)]
            sq = sb("sq", [128, 256]); rn = sb("rn", [128, 256])
            qk_ = [sb("qk_%d" % i, [128, 2, 128]) for i in range(2)]
            tok = [sb("tok%d" % i, [128, 3, 128]) for i in range(2)]
            dd = [sb("dd%d" % i, [128, 4, 128]) for i in range(2)]
            qd = [sb("qd%d" % i, [128, 128]) for i in range(2)]
            qkT = [sb("qkT%d" % i, [128, 128]) for i in range(2)]
            XY = [sb("XY%d" % i, [128, 2, 128]) for i in range(3)]
            U = [sb("U%d" % i, [128, 128]) for i in range(3)]
            nk = [sb("nk%d" % i, [128, 128]) for i in range(2)]
            vn = [sb("vn%d" % i, [128, 128]) for i in range(2)]
            sg = [sb("sg%d" % i, [128, 128]) for i in range(2)]
            res = [sb("res%d" % i, [128, 128]) for i in range(2)]
            so = sb("so", [128, 4])
            it = 0
            for t in range(NT):
                rows = slice(t * 128, (t + 1) * 128)
                pa_, pb_ = pc(rows, A_BETA, A_BETA + 32)
                P.dma(ba.ap[:], pa_, r=[pb_], w=[ba])
                V("act", "activation", [ba], [tk], out=tk.ap[:, 0, :], in_=ba.ap[:, 0:16], func=AF.Sigmoid)
                V("dve", "tensor_tensor", [ba, dtb], [tk], out=tk.ap[:, 7, :], in0=ba.ap[:, 16:32], in1=dtb.ap[:], op=ALU.add)
                V("act", "activation", [tk], [tk], out=tk.ap[:, 7, :], in_=tk.ap[:, 7, :], func=AF.Exp)
                V("act", "activation", [tk], [tk], out=tk.ap[:, 7, :], in_=tk.ap[:, 7, :], func=AF.Ln, bias=1.0)
                V("dve", "tensor_tensor", [tk, nea], [tk], out=tk.ap[:, 1, :], in0=tk.ap[:, 7, :], in1=nea.ap[:], op=ALU.mult)
                gps = psum()
                mm(gps.ap[:, 0:16], ltri.ap[:], tk.ap[:, 1, :], [ltri, tk], [gps])
                mm(gps.ap[:, 16:32], ones.ap[:], tk.ap[:, 1, :], [ones, tk], [gps])
                V("dve", "tensor_copy", [gps], [tk], out=tk.ap[:, 2, :], in_=gps.ap[:, 0:16])
                V("act", "activation", [gps], [tk], out=tk.ap[:, 3, :], in_=gps.ap[:, 0:16], func=AF.Exp)
                V("dve", "tensor_tensor", [gps, tk], [tk], out=tk.ap[:, 4, :], in0=gps.ap[:, 16:32], in1=tk.ap[:, 2, :], op=ALU.subtract)
                V("act", "activation", [tk], [tk], out=tk.ap[:, 4, :], in_=tk.ap[:, 4, :], func=AF.Exp)
                V("act", "activation", [gps], [tk], out=tk.ap[:, 5, :], in_=gps.ap[:, 16:32], func=AF.Exp)
                V("dve", "tensor_tensor", [tk], [tk], out=tk.ap[:, 6, :], in0=tk.ap[:, 0, :], in1=tk.ap[:, 3, :], op=ALU.mult)
                rps = psum()
                tr(rps.ap[0:16, 0:128], tk.ap[:, 2, :], [tk], [rps])
                tr(rps.ap[0:16, 128:256], tk.ap[:, 0, :], [tk], [rps])
                V("act", "activation", [rps], [rowsT], out=rowsT.ap[:], in_=rps.ap[0:16, 0:256], func=AF.Identity)
                for hh in range(16):
                    a = qkvg[it % 2]; x_ = xc[it % 2]; ac = accs[it % 2]; qk = qk_[it % 2]; tt = tok[it % 2]; d_ = dd[it % 2]
                    qd_ = qd[it % 2]; qkT_ = qkT[it % 2]; nk_ = nk[it % 2]; vn_ = vn[it % 2]; sg_ = sg[it % 2]; rs_ = res[it % 2]
                    it += 1
                    for s3 in range(3):
                        pa_, pb_ = pc(rows, s3 * 2048 + hh * 128, s3 * 2048 + (hh + 1) * 128)
                        P.dma(a.ap[:, s3, :], pa_, r=[pb_], w=[a])
                    pa_, pb_ = pc(rows, A_GATE + hh * 128, A_GATE + (hh + 1) * 128)
                    P.dma(a.ap[:, 3, :], pa_, r=[pb_], w=[a])
                    ps = psum()
                    for s in range(3):
                        tr(ps.ap[:, s * 128:(s + 1) * 128], a.ap[:, s, :], [a], [ps])
                    V("pool", "tensor_copy", [halo], [x_], out=x_.ap[:, :, 0:3], in_=halo.ap[:, hh])
                    V("act", "activation", [ps], [x_], out=x_.ap[:, :, 3:131], in_=ps.ap[:, 0:384].rearrange("p (a b) -> p a b", a=3), func=AF.Identity)
                    V("pool", "tensor_copy", [x_], [halo], out=halo.ap[:, hh], in_=x_.ap[:, :, 128:131])
                    for s in range(3):
                        ci = s * 16 + hh
                        eng = "dve" if s < 2 else "pool"
                        V(eng, "tensor_scalar", [x_, cw], [ac], out=ac.ap[:, s, :], in0=x_.ap[:, s, 0:128], scalar1=cw.ap[:, ci, 0:1], scalar2=None, op0=ALU.mult)
                        for jj in range(1, 4):
                            if eng == "dve":
                                V("dve", "scalar_tensor_tensor", [x_, cw, ac], [ac], out=ac.ap[:, s, :], in0=x_.ap[:, s, jj:jj + 128], scalar=cw.ap[:, ci, jj:jj + 1],
                                  in1=ac.ap[:, s, :], op0=ALU.mult, op1=ALU.add)
                            else:
                                V("pool", "tensor_scalar", [x_, cw], [d_], out=d_.ap[:, 0, :], in0=x_.ap[:, s, jj:jj + 128], scalar1=cw.ap[:, ci, jj:jj + 1], scalar2=None, op0=ALU.mult)
                                V("pool", "tensor_tensor", [d_, ac], [ac], out=ac.ap[:, s, :], in0=ac.ap[:, s, :], in1=d_.ap[:, 0, :], op=ALU.add)
                    V("act", "activation", [ac], [ac], out=ac.ap[:], in_=ac.ap[:], func=AF.Silu)
                    V("pool", "tensor_tensor", [ac], [sq], out=sq.ap[:], in0=ac.ap[:, 0:2, :].rearrange("p a b -> p (a b)"), in1=ac.ap[:, 0:2, :].rearrange("p a b -> p (a b)"), op=ALU.mult)
                    sps = psum()
                    mm(sps.ap[:, 0:256], ones.ap[:], sq.ap[:], [ones, sq], [sps])
                    V("dve", "tensor_scalar", [sps], [rn], out=rn.ap[:], in0=sps.ap[:, 0:256], scalar1=EPS, scalar2=None, op0=ALU.add)
                    V("act", "activation", [rn], [rn], out=rn.ap[:], in_=rn.ap[:], func=AF.Sqrt)
                    V("dve", "reciprocal", [rn], [rn], out=rn.ap[:], in_=rn.ap[:])
                    V("dve", "scalar_tensor_tensor", [ac, rn], [qk], out=qk.ap[:, 0, :], in0=ac.ap[:, 0, :], scalar=128.0 ** -0.5, in1=rn.ap[:, 0:128], op0=ALU.mult, op1=ALU.mult)
                    V("pool", "tensor_tensor", [ac, rn], [qk], out=qk.ap[:, 1, :], in0=ac.ap[:, 1, :], in1=rn.ap[:, 128:256], op=ALU.mult)
                    tps = psum()
                    tr(tps.ap[:, 0:128], qk.ap[:, 1, :], [qk], [tps])
                    tr(tps.ap[:, 128:256], ac.ap[:, 2, :], [ac], [tps])
                    V("dve", "tensor_scalar", [tps, tk], [tt], out=tt.ap[:, 0, :], in0=tps.ap[:, 0:128], scalar1=tk.ap[:, 6, hh:hh + 1], scalar2=None, op0=ALU.mult)
                    V("act", "activation", [tps, tk], [tt], out=tt.ap[:, 1, :], in_=tps.ap[:, 0:128], func=AF.Copy, scale=tk.ap[:, 4, hh:hh + 1])
                    V("dve", "tensor_scalar", [tps, tk], [tt], out=tt.ap[:, 2, :], in0=tps.ap[:, 128:256], scalar1=tk.ap[:, 0, hh:hh + 1], scalar2=None, op0=ALU.mult)
                    bps = psum()
                    mm(bps.ap[:, 0:128], sel.ap[:, hh * 128:(hh + 1) * 128], rowsT.ap[:, 0:128], [sel, rowsT], [bps])
                    mm(bps.ap[:, 128:256], sel.ap[:, hh * 128:(hh + 1) * 128], rowsT.ap[:, 128:256], [sel, rowsT], [bps])
                    V("dve", "tensor_scalar", [bps, tk], [d_], out=d_.ap[:, 0, :], in0=bps.ap[:, 0:128], scalar1=tk.ap[:, 2, hh:hh + 1], scalar2=0.0, op0=ALU.subtract, op1=ALU.min)
                    V("act", "activation", [d_], [d_], out=d_.ap[:, 0, :], in_=d_.ap[:, 0, :], func=AF.Exp)
                    V("pool", "tensor_tensor", [d_, ltri], [d_], out=d_.ap[:, 1, :], in0=d_.ap[:, 0, :], in1=ltri.ap[:], op=ALU.mult)
                    V("dve", "scalar_tensor_tensor", [bps, d_], [d_], out=d_.ap[:, 2, :], in0=bps.ap[:, 128:256], scalar=-1.0, in1=d_.ap[:, 0, :], op0=ALU.mult, op1=ALU.mult)
                    V("pool", "tensor_tensor", [d_, lstr], [d_], out=d_.ap[:, 2, :], in0=d_.ap[:, 2, :], in1=lstr.ap[:], op=ALU.mult)
                    V("act", "activation", [bps], [d_], out=d_.ap[:, 3, :], in_=bps.ap[:, 0:128], func=AF.Exp)
                    V("pool", "tensor_tensor", [qk, d_], [qd_], out=qd_.ap[:], in0=qk.ap[:, 0, :], in1=d_.ap[:, 3, :], op=ALU.mult)
                    kps = psum()
                    mm(kps.ap[:, 0:128], qk.ap[:, 1, :], qk.ap[:, 1, :], [qk], [kps])
                    mm(kps.ap[:, 128:256], qk.ap[:, 1, :], qk.ap[:, 0, :], [qk], [kps])
                    X = XY[0]
                    V("dve", "tensor_tensor", [kps, d_], [X], out=X.ap[:, 0, :], in0=kps.ap[:, 0:128], in1=d_.ap[:, 2, :], op=ALU.mult)
                    V("dve", "tensor_tensor", [kps, d_], [qkT_], out=qkT_.ap[:], in0=kps.ap[:, 128:256], in1=d_.ap[:, 1, :], op=ALU.mult)
                    yps = psum()
                    tr(yps.ap[:, 0:128], X.ap[:, 0, :], [X], [yps])
                    V("act", "activation", [yps], [X], out=X.ap[:, 1, :], in_=yps.ap[:, 0:128], func=AF.Identity)
                    u_ = U[0]
                    V("pool", "tensor_tensor", [X, ident], [u_], out=u_.ap[:], in0=X.ap[:, 0, :], in1=ident.ap[:], op=ALU.add)
                    for s in range(6):
                        Xn = XY[(s + 1) % 3]
                        xps = psum()
                        mm(xps.ap[:, 0:128], X.ap[:, 1, :], X.ap[:, 0, :], [X], [xps])
                        mm(xps.ap[:, 128:256], X.ap[:, 0, :], X.ap[:, 1, :], [X], [xps])
                        V("act", "activation", [xps], [Xn], out=Xn.ap[:], in_=xps.ap[:, 0:256].rearrange("p (a b) -> p a b", a=2), func=AF.Identity)
                        ups = psum()
                        mm(ups.ap[:, 0:128], Xn.ap[:, 1, :], u_.ap[:], [Xn, u_], [ups])
                        un = U[(s + 1) % 3]
                        V("dve", "tensor_tensor", [ups, u_], [un], out=un.ap[:], in0=ups.ap[:, 0:128], in1=u_.ap[:], op=ALU.add)
                        X = Xn; u_ = un
                    cps = psum()
                    mm(cps.ap[:, 0:128], tt.ap[:, 0, :], u_.ap[:], [tt, u_], [cps])
                    V("act", "activation", [cps], [nk_], out=nk_.ap[:], in_=cps.ap[:, 0:128], func=AF.Copy, scale=-1.0)
                    vps = psum()
                    mm(vps.ap[:, 0:128], u_.ap[:], tt.ap[:, 2, :], [u_, tt], [vps], start=True, stop=False)
                    mm(vps.ap[:, 0:128], nk_.ap[:], S.ap[:, hh, :], [nk_, S], [vps], start=False, stop=True)
                    V("act", "activation", [vps], [vn_], out=vn_.ap[:], in_=vps.ap[:, 0:128], func=AF.Identity)
                    ops_ = psum()
                    mm(ops_.ap[:, 0:128], qd_.ap[:], S.ap[:, hh, :], [qd_, S], [ops_], start=True, stop=False)
                    mm(ops_.ap[:, 0:128], qkT_.ap[:], vn_.ap[:], [qkT_, vn_], [ops_], start=False, stop=True)
                    sps2 = psum()
                    mm(sps2.ap[:, 0:128], tt.ap[:, 1, :], vn_.ap[:], [tt, vn_], [sps2])
                    V("dve", "scalar_tensor_tensor", [S, tk, sps2], [S], out=S.ap[:, hh, :], in0=S.ap[:, hh, :], scalar=tk.ap[:, 5, hh:hh + 1], in1=sps2.ap[:, 0:128],
                      op0=ALU.mult, op1=ALU.add)
                    V("act", "activation", [ops_], [rs_, so], out=rs_.ap[:], in_=ops_.ap[:, 0:128], func=AF.Square, accum_out=so.ap[:, 0:1])
                    V("dve", "tensor_scalar", [so], [so], out=so.ap[:, 1:2], in0=so.ap[:, 0:1], scalar1=1.0 / 128, scalar2=EPS, op0=ALU.mult, op1=ALU.add)
                    V("act", "activation", [so], [so], out=so.ap[:, 1:2], in_=so.ap[:, 1:2], func=AF.Sqrt)
                    V("dve", "reciprocal", [so], [so], out=so.ap[:, 2:3], in_=so.ap[:, 1:2])
                    V("act", "activation", [a], [sg_], out=sg_.ap[:], in_=a.ap[:, 3, :], func=AF.Silu)
                    V("pool", "tensor_tensor", [sg_, gon], [sg_], out=sg_.ap[:], in0=sg_.ap[:], in1=gon.ap[:], op=ALU.mult)
                    V("dve", "scalar_tensor_tensor", [ops_, so, sg_], [rs_], out=rs_.ap[:], in0=ops_.ap[:, 0:128], scalar=so.ap[:, 2:3], in1=sg_.ap[:], op0=ALU.mult, op1=ALU.mult)
                    P.dma(o_d.ap[rows, hh * 128:(hh + 1) * 128], rs_.ap[:], r=[rs_], w=[o_d])
        run_stage(s3a, "a")

        def s4a(sb):
            ots = [sb("ot%d" % i, [128, 5120]) for i in range(2)]
            oss = [sb("os%d" % i, [128, 40, 128], BF16) for i in range(2)]
            for t in range(NT):
                ot = ots[t % 2]; os_ = oss[t % 2]
                P.dma(ot.ap[:], o_d.ap[t * 128:(t + 1) * 128, :], r=[o_d], w=[ot])
                for c4 in range(10):
                    pt = psum()
                    for i in range(4):
                        c = c4 * 4 + i
                        tr(pt.ap[:, i * 128:(i + 1) * 128], ot.ap[:, c * 128:(c + 1) * 128], [ot], [pt])
                    if c4 % 2 == 0:
                        V("act", "activation", [pt], [os_], out=os_.ap[:, c4 * 4:c4 * 4 + 4, :], in_=pt.ap[:].rearrange("p (a b) -> p a b", a=4), func=AF.Identity)
                    else:
                        V("dve", "tensor_copy", [pt], [os_], out=os_.ap[:, c4 * 4:c4 * 4 + 4, :], in_=pt.ap[:].rearrange("p (a b) -> p a b", a=4))
                P.dma(oT_d.ap[t], os_.ap[:], r=[os_], w=[oT_d])
        run_stage(s4a, "t")

        def s4b(sb):
            wts = [sb("wbt%d" % i, [128, 40, 512], BF16) for i in range(2)]
            ats = [sb("abt%d" % i, [128, 40, 128], BF16) for i in range(3)]
            gts = [sb("gts%d" % i, [128, 3, 512]) for i in range(2)]
            ysb = [sb("ysb%d" % i, [128, 512]) for i in range(2)]
            ysT = [sb("ysT%d" % i, [128, 4, 128], BF16) for i in range(2)]
            wvv = Wl["w_br"].rearrange("(c p) n -> p c n", p=128)
            branches = ((0, 16), (16, 8), (24, 16))
            ac = 0
            for nb in range(8):
                n0 = nb * 512
                wt = wts[nb % 2]
                for k4 in range(0, 40, 8):
                    P.dma(wt.ap[:, k4:k4 + 8, :], wvv[:, k4:k4 + 8, n0:n0 + 512], w=[wt], q="pool")
                for t in range(NT):
                    at = ats[ac % 3]; g = gts[ac % 2]; y = ysb[ac % 2]; yt = ysT[ac % 2]; ac += 1
                    rows = slice(t * 128, (t + 1) * 128)
                    P.dma(at.ap[:], oT_d.ap[t], r=[oT_d], w=[at])
                    for bi in range(3):
                        c0 = MERGE + bi * D + n0
                        pa_, pb_ = pc(rows, c0, c0 + 512)
                        P.dma(g.ap[:, bi, :], pa_, r=[pb_], w=[g])
                    V("act", "activation", [g], [g], out=g.ap[:], in_=g.ap[:], func=AF.Sigmoid)
                    pss = []
                    for bi, (k0, kn) in enumerate(branches):
                        ps = psum(); pss.append(ps)
                        for k in range(kn):
                            mm(ps.ap[:], at.ap[:, k0 + k, :], wt.ap[:, k0 + k, :], [at, wt], [ps], start=(k == 0), stop=(k == kn - 1))
                    V("dve", "tensor_tensor", [pss[0], g], [y], out=y.ap[:], in0=pss[0].ap[:], in1=g.ap[:, 0, :], op=ALU.mult)
                    V("dve", "tensor_tensor", [pss[1], g], [g], out=g.ap[:, 1, :], in0=pss[1].ap[:], in1=g.ap[:, 1, :], op=ALU.mult)
                    V("dve", "tensor_tensor", [pss[2], g], [g], out=g.ap[:, 2, :], in0=pss[2].ap[:], in1=g.ap[:, 2, :], op=ALU.mult)
                    V("pool", "tensor_tensor", [y, g], [y], out=y.ap[:], in0=y.ap[:], in1=g.ap[:, 1, :], op=ALU.add)
                    V("pool", "tensor_tensor", [y, g], [y], out=y.ap[:], in0=y.ap[:], in1=g.ap[:, 2, :], op=ALU.add)
                    pt = psum()
                    for j in range(4):
                        tr(pt.ap[:, j * 128:(j + 1) * 128], y.ap[:, j * 128:(j + 1) * 128], [y], [pt])
                    V("act", "activation", [pt], [yt], out=yt.ap[:], in_=pt.ap[:].rearrange("p (a b) -> p a b", a=4), func=AF.Identity)
                    P.dma(yT_d.ap[t][:, nb * 4:nb * 4 + 4, :], yt.ap[:], r=[yt], w=[yT_d])
        run_stage(s4b, "m")

        if dbg is None or "o" in dbg:
            gemm_stage(yT_d, KC, Wl["w_out"], D, ob_d, "o")

        def s4d(sb):
            obv = ob_d.ap.rearrange("(n p) d -> n p d", p=128)
            nxv = nxt_x.ap.rearrange("(n p) d -> n p d", p=128)
            ots = [sb("fo%d" % i, [128, D]) for i in range(2)]
            xts = [sb("fx%d" % i, [128, D]) for i in range(2)]
            junk = sb("fjunk", [128, D])
            for t in range(NT):
                ot = ots[t % 2]; xt = xts[t % 2]
                P.dma(ot.ap[:], obv[t], r=[ob_d], w=[ot])
                P.dma(xt.ap[:], xv[t], r=[cur_x], w=[xt])
                V("act", "activation", [ot], [junk, ss], out=junk.ap[:], in_=ot.ap[:], func=AF.Square, accum_out=ss.ap[:, 0:1])
                rsqrt_to(rs, ss, 1.0 / D, 1)
                V("dve", "scalar_tensor_tensor", [ot, rs, ggB], [ot], out=ot.ap[:], in0=ot.ap[:], scalar=rs.ap[:, 0:1], in1=ggB.ap[:], op0=ALU.mult, op1=ALU.mult)
                V("pool", "tensor_tensor", [ot, xt], [ot], out=ot.ap[:], in0=ot.ap[:], in1=xt.ap[:], op=ALU.add)
                P.dma(nxv[t], ot.ap[:], r=[ot], w=[nxt_x])
        run_stage(s4d, "f")
        cur_x = nxt_x

    stack.close()
    return nc


def _consts():
    ident = np.eye(128, dtype=np.float32)
    s = np.arange(128)
    ltri = (s[:, None] <= s[None, :]).astype(np.float32)
    lstr = (s[:, None] < s[None, :]).astype(np.float32)
    sel = np.zeros((16, 16, 128), np.float32)
    for h in range(16):
        sel[h, h, :] = 1.0
    slopes = 2.0 ** (-8.0 * np.arange(1, 25, dtype=np.float32) / 24)
    i = np.arange(128)[:, None]
    j = np.arange(256)[None, :]
    rel = i + 128 - j
    valid = (rel >= 0) & (rel <= 128)
    bm = np.zeros((24, 128, 256), np.float32)
    for h in range(24):
        d = (1, 4, 16)[h // 8]
        bm[h] = np.where(valid, -slopes[h] * (d * rel).astype(np.float32), NEG)
    return dict(ident=ident, ltri=ltri, lstr=lstr, sel=sel.reshape(16, 2048), bmask=bm)


def _layer_inputs(inp, l):
    f = np.float32
    def fm(v, n):
        return np.ascontiguousarray(v.reshape(n, 128).T).astype(f)
    def bc(v):
        return np.ascontiguousarray(np.broadcast_to(v[None, :], (128, v.shape[0]))).astype(f)
    b_ada = inp["b_ada"][l]
    conv = inp["conv_a"][l]
    convT = np.ascontiguousarray(conv.reshape(4, 48, 128).transpose(2, 1, 0))
    w_br = np.concatenate([inp["w_br_a"][l], inp["w_br_b"][l], inp["w_br_c"][l]], axis=0)
    return {
        "w_ada%d" % l: inp["w_ada"][l], "b_adaT%d" % l: fm(b_ada[0:8192], 64), "b_gateB%d" % l: bc(b_ada[8192:]),
        "g_preT%d" % l: fm(inp["g_pre"][l], 32), "g_postB%d" % l: bc(inp["g_post"][l]),
        "w_in%d" % l: inp["w_in"][l], "convT%d" % l: convT,
        "a_logB%d" % l: bc(inp["a_log"][l]), "dt_biasB%d" % l: bc(inp["dt_bias"][l]),
        "g_onormB%d" % l: bc(inp["g_onorm_a"][l]),
        "ln_gB%d" % l: bc(inp["ln_c_g"][l]), "ln_bB%d" % l: bc(inp["ln_c_b"][l]),
        "w_spT%d" % l: np.ascontiguousarray(inp["w_spatial"][l].transpose(2, 0, 1)),
        "b_spT%d" % l: np.ascontiguousarray(inp["b_spatial"][l].T),
        "w_br%d" % l: w_br, "w_out%d" % l: inp["w_out"][l],
    }


def kernel(**inputs):
    inp = {k: np.asarray(v) for k, v in inputs.items()}
    x = inp["x"]
    B, T, _ = x.shape
    L = inp["w_in"].shape[0]
    nc = build(T, L)
    shared = _consts()
    for l in range(L):
        shared.update(_layer_inputs(inp, l))
    in_maps = []
    for b in range(B):
        m = dict(shared)
        m["x"] = np.ascontiguousarray(x[b])
        m["cT"] = np.ascontiguousarray(inp["c"][b].reshape(KC, 128).T)
        in_maps.append(m)
    res = run_bass_kernel_spmd(nc, in_maps, core_ids=list(range(B)))
    return np.stack([res.results[b]["out"] for b in range(B)], axis=0).astype(np.float32)
```

```python
import numpy as np
import concourse.bass as bass
import concourse.mybir as mybir
from concourse.bass_utils import run_bass_kernel_spmd

F32 = mybir.dt.float32
BF16 = mybir.dt.bfloat16
AF = mybir.ActivationFunctionType
ALU = mybir.AluOpType
AX = mybir.AxisListType

D = 4096
KC = 32
NW = 36896
A_QKV, A_BETA, A_ALPHA, A_GATE = 0, 6144, 6160, 6176
B_QKV, B_GATE = 8224, 17440
C_U, C_V, C_GATE = 18464, 20512, 22560
MERGE = 24608
EPS = 1e-6
NEG = -30000.0


class Buf:
    def __init__(self, ap=None):
        self.ap = ap
        self.w = {}
        self.r = {}


class Prog:
    ENG = ("pe", "act", "dve", "pool", "sp")

    def __init__(self, nc, stack):
        self.nc = nc
        self.q = {e: [] for e in self.ENG}
        self.cnt = {e: 0 for e in self.ENG}
        self.seen = {e: {} for e in self.ENG}
        self.esem = {e: stack.enter_context(nc.semaphore("s_" + e)) for e in self.ENG if e != "sp"}
        self.dsem = {}
        self.dcnt = {}
        self.dnext = {}
        for qn, k in (("sp", 8), ("pool", 4), ("act", 4)):
            self.dsem[qn] = [stack.enter_context(nc.semaphore("d_%s%d" % (qn, i))) for i in range(k)]
            self.dcnt[qn] = [0] * k
            self.dnext[qn] = 0
        self.ninst = 0

    def _waits(self, eng, deps):
        for key, val in deps.items():
            if self.seen[eng].get(key, 0) < val:
                self.seen[eng][key] = val
                sem = key[1]
                self.q[eng].append(lambda e, sem=sem, val=val: e.wait_ge(sem, val))

    def _deps(self, eng, r, w):
        deps = {}
        for b in r:
            for k, v in b.w.items():
                deps[k] = max(deps.get(k, 0), v)
            if getattr(b, "psum", False):
                for k, v in b.r.items():
                    if k[0] != "e_" + eng:
                        deps[k] = max(deps.get(k, 0), v)
        for b in w:
            for dct in (b.w, b.r):
                for k, v in dct.items():
                    deps[k] = max(deps.get(k, 0), v)
        if eng == "pe":
            deps = {k: v for k, v in deps.items() if k[0] != "e_pe"}
        return deps

    def _mark(self, tok, r, w):
        k, v = tok
        for b in r:
            b.r[k] = max(b.r.get(k, 0), v)
        for b in w:
            b.w[k] = max(b.w.get(k, 0), v)

    def op(self, eng, fn, r=(), w=(), sig=True):
        self.ninst += 1
        self._waits(eng, self._deps(eng, r, w))
        sem = self.esem[eng]
        if sig:
            self.cnt[eng] += 1
            self.q[eng].append(lambda e, fn=fn, sem=sem: fn(e).then_inc(sem, 1))
            tok = (("e_" + eng, sem), self.cnt[eng])
        else:
            self.q[eng].append(lambda e, fn=fn: fn(e))
            tok = (("e_" + eng, sem), self.cnt[eng] + 1)
        self._mark(tok, r, w)

    def dma(self, out, in_, r=(), w=(), q="sp", **kw):
        self.ninst += 1
        i = self.dnext[q]
        self.dnext[q] = (i + 1) % len(self.dsem[q])
        sem = self.dsem[q][i]
        key = ("d_%s%d" % (q, i), sem)
        deps = self._deps(q, r, w)
        if self.dcnt[q][i] > 0:
            deps[key] = max(deps.get(key, 0), 16 * self.dcnt[q][i])
        self._waits(q, deps)
        self.dcnt[q][i] += 1
        self.q[q].append(lambda e, out=out, in_=in_, sem=sem, kw=kw: e.dma_start(out=out, in_=in_, **kw).then_inc(sem, 16))
        self._mark((key, 16 * self.dcnt[q][i]), r, w)

    def barrier(self):
        deps = {}
        for e, s in self.esem.items():
            if self.cnt[e] > 0:
                deps[("e_" + e, s)] = self.cnt[e]
        for qn in self.dsem:
            for i, s in enumerate(self.dsem[qn]):
                if self.dcnt[qn][i] > 0:
                    deps[("d_%s%d" % (qn, i), s)] = 16 * self.dcnt[qn][i]
        for e in self.ENG:
            self._waits(e, dict(deps))

    def clear(self):
        self.q = {e: [] for e in self.ENG}

    def finish(self, bufs):
        deps = {}
        for b in bufs:
            for k, v in b.w.items():
                deps[k] = max(deps.get(k, 0), v)
        self._waits("sp", deps)

    def emit(self, block):
        nc = self.nc

        @block.tensor
        def _(e):
            for f in self.q["pe"]:
                f(e)

        @block.scalar
        def _(e):
            for f in self.q["act"]:
                f(e)

        @block.vector
        def _(e):
            for f in self.q["dve"]:
                f(e)

        @block.gpsimd
        def _(e):
            for f in self.q["pool"]:
                f(e)

        @block.sync
        def _(e):
            for f in self.q["sp"]:
                f(e)


def build(T, L, dbg=None):
    from contextlib import ExitStack
    nc = bass.Bass("TRN2", target_bir_lowering=False)
    NT = T // 128
    stack = ExitStack()
    uid = [0]

    def din(name, shape, dt=F32):
        return nc.dram_tensor(name, list(shape), dt, kind="ExternalInput").ap()

    def dscr(name, shape, dt=F32, kind=None):
        if kind is None:
            return Buf(nc.dram_tensor(name, list(shape), dt).ap())
        return Buf(nc.dram_tensor(name, list(shape), dt, kind=kind).ap())

    class Lazy(dict):
        def __init__(self, shapes):
            dict.__init__(self)
            self.shapes = shapes

        def __missing__(self, k):
            nm, shp = self.shapes[k]
            self[k] = din(nm, shp)
            return self[k]

    x_in = din("x", [T, D]) if (dbg is None or "1" in dbg or "f" in dbg) else None
    cT_in = din("cT", [128, KC])
    out_ap = nc.dram_tensor("out", [T, D], F32, kind="ExternalOutput").ap()
    ident_in = din("ident", [128, 128])
    ltri_in = din("ltri", [128, 128])
    lstr_in = din("lstr", [128, 128])
    sel_in = din("sel", [16, 16 * 128])
    bmask_in = din("bmask", [24, 128, 256])
    W = []
    for l in range(L):
        W.append(Lazy(dict(
            w_ada=("w_ada%d" % l, [D, 3 * D]), b_adaT=("b_adaT%d" % l, [128, 64]),
            b_gateB=("b_gateB%d" % l, [128, D]),
            g_preT=("g_preT%d" % l, [128, KC]), g_postB=("g_postB%d" % l, [128, D]),
            w_in=("w_in%d" % l, [D, NW]), convT=("convT%d" % l, [128, 48, 4]),
            a_logB=("a_logB%d" % l, [128, 16]), dt_biasB=("dt_biasB%d" % l, [128, 16]),
            g_onormB=("g_onormB%d" % l, [128, 128]),
            ln_gB=("ln_gB%d" % l, [128, 2048]), ln_bB=("ln_bB%d" % l, [128, 2048]),
            w_spT=("w_spT%d" % l, [128, 16, 128]), b_spT=("b_spT%d" % l, [128, 16]),
            w_br=("w_br%d" % l, [5120, D]), w_out=("w_out%d" % l, [D, D]),
        )))
    hT_d = dscr("hT_d", [NT, 128, KC, 128], BF16, kind=("ExternalInput" if (dbg is not None and "1" not in dbg and "g" in dbg) else None))
    REG = [(0, 6144), (6144, 8224), (8224, 11296), (11296, 14368), (14368, 17440), (17440, 18464),
           (18464, 24608), (24608, 28704), (28704, 32800), (32800, 36896)]
    p_in = "ExternalInput" if (dbg is not None and "g" not in dbg) else None
    P_regs = [[a, b, None, i] for i, (a, b) in enumerate(REG)]

    def preg(e):
        if e[2] is None:
            e[2] = dscr("P_d%d" % e[3], [T, e[1] - e[0]], kind=p_in)
        return e[2]

    def pc(rows, c0, c1):
        for e in P_regs:
            if e[0] <= c0 and c1 <= e[1]:
                buf = preg(e)
                return buf.ap[rows, c0 - e[0]:c1 - e[0]], buf
        raise AssertionError((c0, c1))

    def p_store(t, n0, nw, s):
        rows = slice(t * 128, (t + 1) * 128)
        for e in P_regs:
            a, b = e[0], e[1]
            lo = max(a, n0); hi = min(b, n0 + nw)
            if lo < hi:
                buf = preg(e)
                P.dma(buf.ap[rows, lo - a:hi - a], s.ap[:, lo - n0:hi - n0], r=[s], w=[buf])
    o_kind = None
    if dbg is not None:
        o_kind = "ExternalOutput" if any(c in dbg for c in "abc") else "ExternalInput"
    o_d = dscr("o_d", [T, 5120], kind=o_kind)
    og_d = dscr("og_d", [3, T, 8, 132], kind=("ExternalOutput" if (dbg is not None and "b" in dbg) else None))
    oT_d = dscr("oT_d", [NT, 128, 40, 128], BF16)
    yT_d = dscr("yT_d", [NT, 128, KC, 128], BF16)
    ob_d = dscr("ob_d", [T, D])
    x1_d = dscr("x1_d", [T, D])
    xin_b = Buf(x_in)
    if dbg is not None and "f" not in dbg:
        pass
    out_b = Buf(out_ap)

    P = Prog(nc, stack)

    def gsb(name, shape, dt=F32):
        return Buf(stack.enter_context(nc.sbuf_tensor("g_" + name, list(shape), dt)))

    psb = [Buf(stack.enter_context(nc.psum_tensor("ps%d" % i, [128, 512], F32))) for i in range(8)]
    for b_ in psb:
        b_.psum = True
    pcount = [0]

    def psum():
        pcount[0] += 1
        return psb[pcount[0] % 8]

    import os as _os
    kstop = int(_os.environ.get("KSTOP", "99"))
    nstage = [0]

    def run_stage(fn, letter="k"):
        nstage[0] += 1
        if nstage[0] > kstop:
            return
        if dbg is not None and letter != "k" and letter not in dbg:
            return
        with ExitStack() as st:
            def sb(name, shape, dt=F32):
                uid[0] += 1
                return Buf(st.enter_context(nc.sbuf_tensor("t_%s_%d" % (name, uid[0]), list(shape), dt)))
            fn(sb)
            P.barrier()
            with nc.Block() as block:
                P.emit(block)
            P.clear()

    ident = gsb("ident", [128, 128]); ltri = gsb("ltri", [128, 128]); lstr = gsb("lstr", [128, 128])
    sel = gsb("sel", [16, 2048]); ones = gsb("ones", [128, 128])
    modT = gsb("modT", [128, 64]); gmod = gsb("gmod", [128, KC]); ggB = gsb("ggB", [128, D])
    condT = gsb("condT", [128, KC])
    ss = gsb("ss", [128, 4]); rs = gsb("rs", [128, 4])

    def rsqrt_to(dst, src, scale, n):
        P.op("dve", lambda e: e.tensor_scalar(out=dst.ap[:, 0:n], in0=src.ap[:, 0:n], scalar1=scale, scalar2=EPS,
                                              op0=ALU.mult, op1=ALU.add), r=[src], w=[dst])
        P.op("act", lambda e: e.activation(out=dst.ap[:, 0:n], in_=dst.ap[:, 0:n], func=AF.Sqrt), r=[dst], w=[dst])
        P.op("dve", lambda e: e.reciprocal(out=dst.ap[:, 0:n], in_=dst.ap[:, 0:n]), r=[dst], w=[dst])

    def mm(out, lhsT, rhs, r, w, start=True, stop=True, sig=None):
        P.op("pe", lambda e: e.matmul(out, lhsT=lhsT, rhs=rhs, start=start, stop=stop), r=r, w=w,
             sig=stop if sig is None else sig)

    def tr(out, in_, r, w):
        P.op("pe", lambda e: e.transpose(out, in_, ident.ap[:]), r=list(r) + [ident], w=w)

    def V(eng, fname, r, w, **kw):
        P.op(eng, lambda e: getattr(e, fname)(**kw), r=r, w=w)

    def s_const(sb):
        cT = sb("cT", [128, KC])
        P.dma(ident.ap[:], ident_in[:, :], w=[ident])
        P.dma(ltri.ap[:], ltri_in[:, :], w=[ltri])
        P.dma(lstr.ap[:], lstr_in[:, :], w=[lstr])
        P.dma(sel.ap[:], sel_in[:, :], w=[sel])
        V("pool", "memset", [], [ones], ap=ones.ap[:], constant=1.0)
        P.dma(cT.ap[:], cT_in[:, :], w=[cT])
        V("act", "activation", [cT], [condT], out=condT.ap[:], in_=cT.ap[:], func=AF.Silu)
    run_stage(s_const)

    cur_x = xin_b
    for l in range(L):
        Wl = W[l]
        nxt_x = out_b if l == L - 1 else x1_d
        xv = cur_x.ap.rearrange("(n p) d -> n p d", p=128) if cur_x.ap is not None else None

        def s0(sb):
            wa = [sb("wa%d" % i, [128, KC, 128]) for i in range(2)]
            condB = sb("condB", [128, KC, 128])
            b_adaT = sb("b_adaT", [128, 64]); g_preT = sb("g_preT", [128, KC]); gpost = sb("gpost", [128, D])
            for k in range(KC):
                V("dve", "tensor_scalar", [ones, condT], [condB], out=condB.ap[:, k, :], in0=ones.ap[:], scalar1=condT.ap[:, k:k + 1],
                  scalar2=None, op0=ALU.mult)
            P.dma(b_adaT.ap[:], Wl["b_adaT"][:, :], w=[b_adaT])
            P.dma(g_preT.ap[:], Wl["g_preT"][:, :], w=[g_preT])
            P.dma(ggB.ap[:], Wl["b_gateB"][:, :], w=[ggB])
            P.dma(gpost.ap[:], Wl["g_postB"][:, :], w=[gpost])
            wv = Wl["w_ada"].rearrange("(c p) n -> p c n", p=128)
            mps = psum()
            for j in range(96):
                wt = wa[j % 2]
                P.dma(wt.ap[:], wv[:, :, j * 128:(j + 1) * 128], w=[wt])
                if j < 64:
                    for k in range(KC):
                        mm(mps.ap[:, j:j + 1], wt.ap[:, k, :], condT.ap[:, k:k + 1], [wt, condT], [mps], start=(k == 0), stop=(k == KC - 1))
                else:
                    gp = psum()
                    if gp is mps:
                        gp = psum()
                    for k in range(KC):
                        mm(gp.ap[:, 0:128], condB.ap[:, k, :], wt.ap[:, k, :], [wt, condB], [gp], start=(k == 0), stop=(k == KC - 1))
                    c0 = (j - 64) * 128
                    V("dve", "tensor_tensor", [gp, ggB], [ggB], out=ggB.ap[:, c0:c0 + 128], in0=gp.ap[:, 0:128], in1=ggB.ap[:, c0:c0 + 128], op=ALU.add)
            V("dve", "tensor_tensor", [mps, b_adaT], [modT], out=modT.ap[:], in0=mps.ap[:, 0:64], in1=b_adaT.ap[:], op=ALU.add)
            V("dve", "scalar_tensor_tensor", [modT, g_preT], [gmod], out=gmod.ap[:], in0=modT.ap[:, 32:64], scalar=1.0, in1=g_preT.ap[:],
              op0=ALU.add, op1=ALU.mult)
            V("dve", "tensor_tensor", [ggB, gpost], [ggB], out=ggB.ap[:], in0=ggB.ap[:], in1=gpost.ap[:], op=ALU.mult)
        run_stage(s0, "0")

        def s1(sb):
            xts = [sb("xt%d" % i, [128, D]) for i in range(2)]
            junk = sb("junk", [128, D])
            hss = [sb("hs%d" % i, [128, KC, 128], BF16) for i in range(2)]
            for t in range(NT):
                xt = xts[t % 2]; hs = hss[t % 2]
                P.dma(xt.ap[:], xv[t], r=[cur_x], w=[xt])
                V("act", "activation", [xt], [junk, ss], out=junk.ap[:], in_=xt.ap[:], func=AF.Square, accum_out=ss.ap[:, 0:1])
                rsqrt_to(rs, ss, 1.0 / D, 1)
                V("dve", "tensor_scalar", [xt, rs], [xt], out=xt.ap[:], in0=xt.ap[:], scalar1=rs.ap[:, 0:1], scalar2=None, op0=ALU.mult)
                for c4 in range(8):
                    pt = psum()
                    for i in range(4):
                        c = c4 * 4 + i
                        tr(pt.ap[:, i * 128:(i + 1) * 128], xt.ap[:, c * 128:(c + 1) * 128], [xt], [pt])
                    for i in range(4):
                        c = c4 * 4 + i
                        if i % 2 == 0:
                            V("act", "activation", [pt, gmod, modT], [hs], out=hs.ap[:, c, :], in_=pt.ap[:, i * 128:(i + 1) * 128], func=AF.Identity,
                              scale=gmod.ap[:, c:c + 1], bias=modT.ap[:, c:c + 1])
                        else:
                            V("dve", "tensor_scalar", [pt, gmod, modT], [hs], out=hs.ap[:, c, :], in0=pt.ap[:, i * 128:(i + 1) * 128],
                              scalar1=gmod.ap[:, c:c + 1], scalar2=modT.ap[:, c:c + 1], op0=ALU.mult, op1=ALU.add)
                P.dma(hT_d.ap[t], hs.ap[:], r=[hs], w=[hT_d])
        run_stage(s1, "1")

        def gemm_stage(a_d, kc, wd, n_total, dst, letter):
            def body(sb):
                wts = [sb("wt%d" % i, [128, kc, 512], BF16) for i in range(2)]
                ats = [sb("at%d" % i, [128, kc, 128], BF16) for i in range(4)]
                stg = [sb("stg%d" % i, [128, 512]) for i in range(4)]
                wvv = wd.rearrange("(c p) n -> p c n", p=128)
                nblk = (n_total + 511) // 512
                PF = 3
                work = [(nb, t) for nb in range(nblk) for t in range(NT)]

                def load_w(nb):
                    n0 = nb * 512
                    nw = min(512, n_total - n0)
                    wt = wts[nb % 2]
                    for k4 in range(0, kc, 8):
                        P.dma(wt.ap[:, k4:k4 + 8, 0:nw], wvv[:, k4:k4 + 8, n0:n0 + nw], w=[wt], q="pool")

                def load_a(i):
                    nb, t = work[i]
                    at = ats[i % 4]
                    P.dma(at.ap[:], a_d.ap[t], r=[a_d], w=[at])

                load_w(0)
                for i in range(min(PF, len(work))):
                    load_a(i)
                ec = 0
                for i, (nb, t) in enumerate(work):
                    n0 = nb * 512
                    nw = min(512, n_total - n0)
                    wt = wts[nb % 2]
                    if t == 0 and nb + 1 < nblk:
                        load_w(nb + 1)
                    if i + PF < len(work):
                        load_a(i + PF)
                    at = ats[i % 4]
                    ps = psum()
                    for k in range(kc):
                        mm(ps.ap[:, 0:nw], at.ap[:, k, :], wt.ap[:, k, 0:nw], [at, wt], [ps], start=(k == 0), stop=(k == kc - 1))
                    s = stg[ec % 4]
                    if ec % 2 == 0:
                        V("act", "activation", [ps], [s], out=s.ap[:, 0:nw], in_=ps.ap[:, 0:nw], func=AF.Identity)
                    else:
                        V("dve", "tensor_copy", [ps], [s], out=s.ap[:, 0:nw], in_=ps.ap[:, 0:nw])
                    ec += 1
                    if dst is None:
                        p_store(t, n0, nw, s)
                    else:
                        P.dma(dst.ap[t * 128:(t + 1) * 128, n0:n0 + nw], s.ap[:, 0:nw], r=[s], w=[dst])
            run_stage(body, letter)

        if dbg is None or "g" in dbg:
            gemm_stage(hT_d, KC, Wl["w_in"], NW, None, "g")

        def s3c(sb):
            lng = sb("lng", [128, 2048]); lnb = sb("lnb", [128, 2048])
            wsp = sb("wsp", [128, 16, 128]); wspb = sb("wspb", [128, 16, 128], BF16); bsp = sb("bsp", [128, 16])
            P.dma(lng.ap[:], Wl["ln_gB"][:, :], w=[lng]); P.dma(lnb.ap[:], Wl["ln_bB"][:, :], w=[lnb])
            P.dma(wsp.ap[:], Wl["w_spT"][:, :, :], w=[wsp]); P.dma(bsp.ap[:], Wl["b_spT"][:, :], w=[bsp])
            for g in range(16):
                V("dve", "tensor_tensor", [wsp, ltri], [wspb], out=wspb.ap[:, g, :], in0=wsp.ap[:, g, :], in1=ltri.ap[:], op=ALU.mult)
            uvg = [sb("uvg%d" % i, [128, 3, 2048]) for i in range(2)]
            tmp = sb("tmpc", [128, 2, 2048]); vnb = sb("vnb", [128, 2048], BF16)
            st6 = sb("st6", [128, 4, 6]); mv = sb("mv", [128, 4]); orow = [sb("orow%d" % i, [128, 2048]) for i in range(2)]

            def gelu(xa, ta, eng2):
                V("pool", "tensor_tensor", [uv], [tmp], out=ta, in0=xa, in1=xa, op=ALU.mult)
                V(eng2, "tensor_scalar", [tmp], [tmp], out=ta, in0=ta, scalar1=0.044715, scalar2=1.0, op0=ALU.mult, op1=ALU.add)
                V("pool", "tensor_tensor", [uv, tmp], [tmp], out=ta, in0=ta, in1=xa, op=ALU.mult)
                V("act", "activation", [tmp], [tmp], out=ta, in_=ta, func=AF.Sigmoid, scale=1.5957691216)
                V(eng2, "tensor_tensor", [uv, tmp], [uv], out=xa, in0=xa, in1=ta, op=ALU.mult)
            for t in range(NT):
                uv = uvg[t % 2]; orw = orow[t % 2]
                rows = slice(t * 128, (t + 1) * 128)
                pa_, pb_ = pc(rows, C_U, C_U + 6144)
                P.dma(uv.ap[:], pa_.rearrange("p (a b) -> p a b", a=3), r=[pb_], w=[uv])
                gelu(uv.ap[:, 0, :], tmp.ap[:, 0, :], "dve")
                gelu(uv.ap[:, 1, :], tmp.ap[:, 1, :], "dve")
                for i in range(4):
                    V("dve", "bn_stats", [uv], [st6], out=st6.ap[:, i, :], in_=uv.ap[:, 1, i * 512:(i + 1) * 512])
                V("dve", "bn_aggr", [st6], [mv], out=mv.ap[:, 0:2], in_=st6.ap[:].rearrange("p a b -> p (a b)"))
                rsqrt_to(rs, Buf_view(mv, 1), 1.0, 1)
                V("dve", "tensor_scalar", [uv, mv, rs], [uv], out=uv.ap[:, 1, :], in0=uv.ap[:, 1, :], scalar1=mv.ap[:, 0:1], scalar2=rs.ap[:, 0:1],
                  op0=ALU.subtract, op1=ALU.mult)
                V("pool", "tensor_tensor", [uv, lng], [uv], out=uv.ap[:, 1, :], in0=uv.ap[:, 1, :], in1=lng.ap[:], op=ALU.mult)
                V("dve", "tensor_tensor", [uv, lnb], [vnb], out=vnb.ap[:], in0=uv.ap[:, 1, :], in1=lnb.ap[:], op=ALU.add)
                V("act", "activation", [uv], [uv], out=uv.ap[:, 2, :], in_=uv.ap[:, 2, :], func=AF.Silu)
                V("pool", "tensor_tensor", [uv], [uv], out=uv.ap[:, 0, :], in0=uv.ap[:, 0, :], in1=uv.ap[:, 2, :], op=ALU.mult)
                for g4 in range(4):
                    ps = psum()
                    for i in range(4):
                        g = g4 * 4 + i
                        mm(ps.ap[:, i * 128:(i + 1) * 128], wspb.ap[:, g, :], vnb.ap[:, g * 128:(g + 1) * 128], [wspb, vnb], [ps])
                    for i in range(4):
                        g = g4 * 4 + i
                        V("dve", "scalar_tensor_tensor", [ps, bsp, uv], [orw], out=orw.ap[:, g * 128:(g + 1) * 128], in0=ps.ap[:, i * 128:(i + 1) * 128],
                          scalar=bsp.ap[:, g:g + 1], in1=uv.ap[:, 0, g * 128:(g + 1) * 128], op0=ALU.add, op1=ALU.mult)
                P.dma(o_d.ap[rows, 3072:5120], orw.ap[:], r=[orw], w=[o_d])

        def Buf_view(b, col):
            v = Buf(b.ap[:, col:col + 1]); v.w = b.w; v.r = b.r
            return v
        run_stage(s3c, "c")

        def s3b(sb):
            bm = sb("bm", [128, 24, 256])
            P.dma(bm.ap[:], bmask_in.rearrange("h p j -> p h j"), w=[bm])
            NSB = 4

            def mkalloc(b0, b1):
                st_ = [0]

                def al():
                    st_[0] += 1
                    return psb[b0] if st_[0] % 2 == 1 else psb[b1]
                return al
            bslots = []
            for i in range(NSB):
                bslots.append(dict(
                    psum=mkalloc(2 * i, 2 * i + 1),
                    a=[sb("qkv%d_%d" % (i, k), [128, 3, 128]) for k in range(2)],
                    kt=[sb("kT2_%d_%d" % (i, k), [128, 128], BF16) for k in range(2)],
                    vb=[sb("vbf%d_%d" % (i, k), [128, 128], BF16) for k in range(2)],
                    q_=sb("qTb%d" % i, [128, 128], BF16), s_=sb("sbt%d" % i, [128, 256]), p_=sb("pb%d" % i, [128, 256]),
                    pt_=sb("pT%d" % i, [128, 256], BF16), t_=sb("stb%d" % i, [128, 8]),
                    o_=[sb("oo%d_%d" % (i, k), [128, 132]) for k in range(2)]))
            kb = int(_os.environ.get("KB", "0"))

            def stream(gi, d, j, r, B):
                hidx = gi * 8 + j
                cq = B_QKV + hidx * 128
                nblk = T // (128 * d)
                psum_ = B["psum"]
                q_ = B["q_"]; s_ = B["s_"]; p_ = B["p_"]; pt_ = B["pt_"]; t_ = B["t_"]
                for n in range(nblk):
                    a = B["a"][n % 2]; kt = B["kt"][n % 2]; vb = B["vb"][n % 2]
                    kp = B["kt"][(n - 1) % 2]; vp = B["vb"][(n - 1) % 2]; o_ = B["o_"][n % 2]
                    t0 = n * 128 * d + r
                    for s3 in range(3):
                        pa_, pb_ = pc(slice(t0, t0 + 127 * d + 1, d), cq + s3 * 3072, cq + s3 * 3072 + 128)
                        P.dma(a.ap[:, s3, :], pa_, r=[pb_], w=[a])
                    ps = psum_()
                    tr(ps.ap[:, 0:128], a.ap[:, 0, :], [a], [ps])
                    tr(ps.ap[:, 128:256], a.ap[:, 1, :], [a], [ps])
                    yield
                    V("act", "activation", [ps], [q_], out=q_.ap[:], in_=ps.ap[:, 0:128], func=AF.Identity)
                    V("act", "activation", [ps], [kt], out=kt.ap[:], in_=ps.ap[:, 128:256], func=AF.Identity)
                    V("pool", "tensor_copy", [a], [vb], out=vb.ap[:], in_=a.ap[:, 2, :])
                    sp_ = psum_()
                    k0 = 0 if n > 0 else 128
                    if n > 0:
                        mm(sp_.ap[:, 0:128], q_.ap[:], kp.ap[:], [q_, kp], [sp_])
                    mm(sp_.ap[:, 128:256], q_.ap[:], kt.ap[:], [q_, kt], [sp_])
                    yield
                    V("dve", "scalar_tensor_tensor", [sp_, bm], [s_], out=s_.ap[:, k0:256], in0=sp_.ap[:, k0:256], scalar=128.0 ** -0.5,
                      in1=bm.ap[:, hidx, k0:256], op0=ALU.mult, op1=ALU.add)
                    V("dve", "reduce_max", [s_], [t_], out=t_.ap[:, 0:1], in_=s_.ap[:, k0:256], axis=AX.X)
                    V("dve", "tensor_scalar", [t_], [t_], out=t_.ap[:, 1:2], in0=t_.ap[:, 0:1], scalar1=-1.0, scalar2=None, op0=ALU.mult)
                    yield
                    V("act", "activation", [s_, t_], [p_, t_], out=p_.ap[:, k0:256], in_=s_.ap[:, k0:256], func=AF.Exp, bias=t_.ap[:, 1:2],
                      accum_out=t_.ap[:, 2:3])
                    tp = psum_()
                    if n > 0:
                        tr(tp.ap[:, 0:128], p_.ap[:, 0:128], [p_], [tp])
                    tr(tp.ap[:, 128:256], p_.ap[:, 128:256], [p_], [tp])
                    yield
                    V("act", "activation", [tp], [pt_], out=pt_.ap[:, k0:256], in_=tp.ap[:, k0:256], func=AF.Identity)
                    op_ = psum_()
                    if n > 0:
                        mm(op_.ap[:, 0:128], pt_.ap[:, 0:128], vp.ap[:], [pt_, vp], [op_], start=True, stop=False)
                    mm(op_.ap[:, 0:128], pt_.ap[:, 128:256], vb.ap[:], [pt_, vb], [op_], start=(n == 0), stop=True)
                    V("dve", "reciprocal", [t_], [t_], out=t_.ap[:, 3:4], in_=t_.ap[:, 2:3])
                    V("act", "activation", [t_], [t_], out=t_.ap[:, 4:5], in_=t_.ap[:, 2:3], func=AF.Ln)
                    yield
                    V("dve", "tensor_scalar", [op_, t_], [o_], out=o_.ap[:, 0:128], in0=op_.ap[:, 0:128], scalar1=t_.ap[:, 3:4], scalar2=None, op0=ALU.mult)
                    V("dve", "tensor_tensor", [t_], [o_], out=o_.ap[:, 128:129], in0=t_.ap[:, 4:5], in1=t_.ap[:, 0:1], op=ALU.add)
                    P.dma(og_d.ap[gi, t0:t0 + 127 * d + 1:d, j, :], o_.ap[:], r=[o_], w=[og_d])
                    yield

            streams = [(gi, d, j, r) for gi, d in enumerate((1, 4, 16)) for j in range(8) for r in range(d)]
            if kb == 1:
                streams = []
            sit = iter(streams)
            active = [None] * NSB
            done = False
            while True:
                for i in range(NSB):
                    if active[i] is None and not done:
                        try:
                            gi, d, j, r = next(sit)
                            active[i] = stream(gi, d, j, r, bslots[i])
                        except StopIteration:
                            done = True
                    if active[i] is not None:
                        try:
                            next(active[i])
                        except StopIteration:
                            active[i] = None
                if done and all(a_ is None for a_ in active):
                    break
            kb_skip = (kb == 2)
            ogt = [sb("ogt%d" % i, [128, 3, 8, 132]) for i in range(2)]
            gt = [sb("gtb%d" % i, [128, 1024]) for i in range(2)]
            wgt = sb("wgt", [128, 3, 8]); wmx = sb("wmx", [128, 8]); acc = [sb("accb%d" % i, [128, 1024]) for i in range(2)]
            for t in range(NT if not kb_skip else 0):
                rows = slice(t * 128, (t + 1) * 128)
                og = ogt[t % 2]; g_ = gt[t % 2]; ac = acc[t % 2]
                for gi in range(3):
                    P.dma(og.ap[:, gi], og_d.ap[gi, rows], r=[og_d], w=[og])
                pa_, pb_ = pc(rows, B_GATE, B_GATE + 1024)
                P.dma(g_.ap[:], pa_, r=[pb_], w=[g_])
                V("dve", "tensor_tensor", [og], [wmx], out=wmx.ap[:], in0=og.ap[:, 0, :, 128], in1=og.ap[:, 1, :, 128], op=ALU.max)
                V("dve", "tensor_tensor", [og, wmx], [wmx], out=wmx.ap[:], in0=wmx.ap[:], in1=og.ap[:, 2, :, 128], op=ALU.max)
                for gi in range(3):
                    V("dve", "tensor_tensor", [og, wmx], [wgt], out=wgt.ap[:, gi, :], in0=og.ap[:, gi, :, 128], in1=wmx.ap[:], op=ALU.subtract)
                V("act", "activation", [wgt], [wgt], out=wgt.ap[:], in_=wgt.ap[:], func=AF.Exp)
                V("dve", "tensor_tensor", [wgt], [wmx], out=wmx.ap[:], in0=wgt.ap[:, 0, :], in1=wgt.ap[:, 1, :], op=ALU.add)
                V("dve", "tensor_tensor", [wgt, wmx], [wmx], out=wmx.ap[:], in0=wmx.ap[:], in1=wgt.ap[:, 2, :], op=ALU.add)
                V("dve", "reciprocal", [wmx], [wmx], out=wmx.ap[:], in_=wmx.ap[:])
                for gi in range(3):
                    V("dve", "tensor_tensor", [wgt, wmx], [wgt], out=wgt.ap[:, gi, :], in0=wgt.ap[:, gi, :], in1=wmx.ap[:], op=ALU.mult)
                V("act", "activation", [g_], [g_], out=g_.ap[:], in_=g_.ap[:], func=AF.Silu)
                for j in range(8):
                    cs = slice(j * 128, (j + 1) * 128)
                    V("dve", "tensor_scalar", [og, wgt], [ac], out=ac.ap[:, cs], in0=og.ap[:, 0, j, 0:128], scalar1=wgt.ap[:, 0, j:j + 1], scalar2=None, op0=ALU.mult)
                    for gi in (1, 2):
                        V("dve", "scalar_tensor_tensor", [og, wgt, ac], [ac], out=ac.ap[:, cs], in0=og.ap[:, gi, j, 0:128], scalar=wgt.ap[:, gi, j:j + 1],
                          in1=ac.ap[:, cs], op0=ALU.mult, op1=ALU.add)
                V("pool", "tensor_tensor", [ac, g_], [ac], out=ac.ap[:], in0=ac.ap[:], in1=g_.ap[:], op=ALU.mult)
                P.dma(o_d.ap[rows, 2048:3072], ac.ap[:], r=[ac], w=[o_d])
        run_stage(s3b, "b")

        def s3a(sb):
            NS = 4
            cw = sb("cw", [128, 48, 4]); nea = sb("nea", [128, 16]); dtb = sb("dtb", [128, 16]); gon = sb("gon", [128, 128])
            P.dma(cw.ap[:], Wl["convT"][:, :, :], w=[cw]); P.dma(nea.ap[:], Wl["a_logB"][:, :], w=[nea])
            P.dma(dtb.ap[:], Wl["dt_biasB"][:, :], w=[dtb]); P.dma(gon.ap[:], Wl["g_onormB"][:, :], w=[gon])
            V("act", "activation", [nea], [nea], out=nea.ap[:], in_=nea.ap[:], func=AF.Exp)
            V("dve", "tensor_scalar", [nea], [nea], out=nea.ap[:], in0=nea.ap[:], scalar1=-1.0, scalar2=None, op0=ALU.mult)
            S = [sb("S%d" % h, [128, 128]) for h in range(16)]
            halo = [sb("halo%d" % h, [128, 3, 3]) for h in range(16)]
            for h in range(16):
                V("pool", "memset", [], [S[h]], ap=S[h].ap[:], constant=0.0)
                V("pool", "memset", [], [halo[h]], ap=halo[h].ap[:], constant=0.0)
            bas = [sb("ba%d" % i, [128, 32]) for i in range(2)]
            tks = [sb("tk%d" % i, [128, 8, 16]) for i in range(2)]
            rowsTs = [sb("rowsT%d" % i, [16, 256]) for i in range(2)]
            def mkalloc(b0, b1):
                st_ = [0]

                def al():
                    st_[0] += 1
                    return psb[b0] if st_[0] % 2 == 1 else psb[b1]
                return al
            slots = []
            for i in range(NS):
                slots.append(dict(psum=mkalloc(2 * i, 2 * i + 1),
                    a=sb("qkvg%d" % i, [128, 4, 128]), x_=sb("xc%d" % i, [128, 3, 131]), ac=sb("acc%d" % i, [128, 3, 128]),
                    sq=sb("sq%d" % i, [128, 256]), rn=sb("rn%d" % i, [128, 256]), qk=sb("qk_%d" % i, [128, 2, 128]),
                    tt=sb("tok%d" % i, [128, 3, 128]), d_=sb("dd%d" % i, [128, 4, 128]), cv=sb("cv%d" % i, [128, 128]),
                    qd_=sb("qd%d" % i, [128, 128]), qkT_=sb("qkT%d" % i, [128, 128]),
                    XY=[sb("XY%d_%d" % (i, k), [128, 2, 128]) for k in range(3)], U=[sb("U%d_%d" % (i, k), [128, 128]) for k in range(3)],
                    nk_=sb("nk%d" % i, [128, 128]), vn_=sb("vn%d" % i, [128, 128]), sg_=sb("sg%d" % i, [128, 128]),
                    rs_=sb("res%d" % i, [128, 128]), so=sb("so%d" % i, [128, 4])))

            def head(t, hh, B, tk, rowsT, rows):
                a = B["a"]; x_ = B["x_"]; ac = B["ac"]; sq = B["sq"]; rn = B["rn"]; qk = B["qk"]; tt = B["tt"]; d_ = B["d_"]; cv = B["cv"]
                qd_ = B["qd_"]; qkT_ = B["qkT_"]; XY = B["XY"]; U = B["U"]; nk_ = B["nk_"]; vn_ = B["vn_"]; sg_ = B["sg_"]; rs_ = B["rs_"]; so = B["so"]
                Sh = S[hh]; hl = halo[hh]
                psum = B["psum"]
                for s3 in range(3):
                    pa_, pb_ = pc(rows, s3 * 2048 + hh * 128, s3 * 2048 + (hh + 1) * 128)
                    P.dma(a.ap[:, s3, :], pa_, r=[pb_], w=[a])
                pa_, pb_ = pc(rows, A_GATE + hh * 128, A_GATE + (hh + 1) * 128)
                P.dma(a.ap[:, 3, :], pa_, r=[pb_], w=[a])
                ps = psum()
                for s in range(3):
                    tr(ps.ap[:, s * 128:(s + 1) * 128], a.ap[:, s, :], [a], [ps])
                yield
                V("pool", "tensor_copy", [hl], [x_], out=x_.ap[:, :, 0:3], in_=hl.ap[:])
                V("act", "activation", [ps], [x_], out=x_.ap[:, :, 3:131], in_=ps.ap[:, 0:384].rearrange("p (a b) -> p a b", a=3), func=AF.Identity)
                V("pool", "tensor_copy", [x_], [hl], out=hl.ap[:], in_=x_.ap[:, :, 128:131])
                V("act", "activation", [a], [sg_], out=sg_.ap[:], in_=a.ap[:, 3, :], func=AF.Silu)
                V("pool", "tensor_tensor", [sg_, gon], [sg_], out=sg_.ap[:], in0=sg_.ap[:], in1=gon.ap[:], op=ALU.mult)
                yield
                for s in range(3):
                    ci = s * 16 + hh
                    eng = "dve" if s < 2 else "pool"
                    V(eng, "tensor_scalar", [x_, cw], [ac], out=ac.ap[:, s, :], in0=x_.ap[:, s, 0:128], scalar1=cw.ap[:, ci, 0:1], scalar2=None, op0=ALU.mult)
                    for jj in range(1, 4):
                        if eng == "dve":
                            V("dve", "scalar_tensor_tensor", [x_, cw, ac], [ac], out=ac.ap[:, s, :], in0=x_.ap[:, s, jj:jj + 128], scalar=cw.ap[:, ci, jj:jj + 1],
                              in1=ac.ap[:, s, :], op0=ALU.mult, op1=ALU.add)
                        else:
                            V("pool", "tensor_scalar", [x_, cw], [cv], out=cv.ap[:], in0=x_.ap[:, s, jj:jj + 128], scalar1=cw.ap[:, ci, jj:jj + 1], scalar2=None, op0=ALU.mult)
                            V("pool", "tensor_tensor", [cv, ac], [ac], out=ac.ap[:, s, :], in0=ac.ap[:, s, :], in1=cv.ap[:], op=ALU.add)
                yield
                V("act", "activation", [ac], [ac], out=ac.ap[:], in_=ac.ap[:], func=AF.Silu)
                V("pool", "tensor_tensor", [ac], [sq], out=sq.ap[:], in0=ac.ap[:, 0:2, :].rearrange("p a b -> p (a b)"), in1=ac.ap[:, 0:2, :].rearrange("p a b -> p (a b)"), op=ALU.mult)
                sps = psum()
                mm(sps.ap[:, 0:256], ones.ap[:], sq.ap[:], [ones, sq], [sps])
                yield
                V("dve", "tensor_scalar", [sps], [rn], out=rn.ap[:], in0=sps.ap[:, 0:256], scalar1=EPS, scalar2=None, op0=ALU.add)
                V("act", "activation", [rn], [rn], out=rn.ap[:], in_=rn.ap[:], func=AF.Sqrt)
                V("dve", "reciprocal", [rn], [rn], out=rn.ap[:], in_=rn.ap[:])
                yield
                V("dve", "scalar_tensor_tensor", [ac, rn], [qk], out=qk.ap[:, 0, :], in0=ac.ap[:, 0, :], scalar=128.0 ** -0.5, in1=rn.ap[:, 0:128], op0=ALU.mult, op1=ALU.mult)
                V("pool", "tensor_tensor", [ac, rn], [qk], out=qk.ap[:, 1, :], in0=ac.ap[:, 1, :], in1=rn.ap[:, 128:256], op=ALU.mult)
                tps = psum()
                tr(tps.ap[:, 0:128], qk.ap[:, 1, :], [qk], [tps])
                tr(tps.ap[:, 128:256], ac.ap[:, 2, :], [ac], [tps])
                bps = tps
                mm(bps.ap[:, 256:384], sel.ap[:, hh * 128:(hh + 1) * 128], rowsT.ap[:, 0:128], [sel, rowsT], [bps])
                mm(bps.ap[:, 384:512], sel.ap[:, hh * 128:(hh + 1) * 128], rowsT.ap[:, 128:256], [sel, rowsT], [bps])
                kps = psum()
                mm(kps.ap[:, 0:128], qk.ap[:, 1, :], qk.ap[:, 1, :], [qk], [kps])
                mm(kps.ap[:, 128:256], qk.ap[:, 1, :], qk.ap[:, 0, :], [qk], [kps])
                yield
                V("dve", "tensor_scalar", [tps, tk], [tt], out=tt.ap[:, 0, :], in0=tps.ap[:, 0:128], scalar1=tk.ap[:, 6, hh:hh + 1], scalar2=None, op0=ALU.mult)
                V("dve", "tensor_scalar", [tps, tk], [tt], out=tt.ap[:, 1, :], in0=tps.ap[:, 0:128], scalar1=tk.ap[:, 4, hh:hh + 1], scalar2=None, op0=ALU.mult)
                V("dve", "tensor_scalar", [tps, tk], [tt], out=tt.ap[:, 2, :], in0=tps.ap[:, 128:256], scalar1=tk.ap[:, 0, hh:hh + 1], scalar2=None, op0=ALU.mult)
                V("dve", "tensor_scalar", [bps, tk], [d_], out=d_.ap[:, 0, :], in0=bps.ap[:, 256:384], scalar1=tk.ap[:, 2, hh:hh + 1], scalar2=0.0, op0=ALU.subtract, op1=ALU.min)
                yield
                V("act", "activation", [d_], [d_], out=d_.ap[:, 0, :], in_=d_.ap[:, 0, :], func=AF.Exp)
                V("act", "activation", [bps], [d_], out=d_.ap[:, 3, :], in_=bps.ap[:, 256:384], func=AF.Exp)
                yield
                V("pool", "tensor_tensor", [d_, ltri], [d_], out=d_.ap[:, 1, :], in0=d_.ap[:, 0, :], in1=ltri.ap[:], op=ALU.mult)
                V("dve", "scalar_tensor_tensor", [bps, d_], [d_], out=d_.ap[:, 2, :], in0=bps.ap[:, 384:512], scalar=-1.0, in1=d_.ap[:, 0, :], op0=ALU.mult, op1=ALU.mult)
                V("pool", "tensor_tensor", [d_, lstr], [d_], out=d_.ap[:, 2, :], in0=d_.ap[:, 2, :], in1=lstr.ap[:], op=ALU.mult)
                V("pool", "tensor_tensor", [qk, d_], [qd_], out=qd_.ap[:], in0=qk.ap[:, 0, :], in1=d_.ap[:, 3, :], op=ALU.mult)
                yield
                X = XY[0]
                V("dve", "tensor_tensor", [kps, d_], [X], out=X.ap[:, 0, :], in0=kps.ap[:, 0:128], in1=d_.ap[:, 2, :], op=ALU.mult)
                V("dve", "tensor_tensor", [kps, d_], [qkT_], out=qkT_.ap[:], in0=kps.ap[:, 128:256], in1=d_.ap[:, 1, :], op=ALU.mult)
                yps = psum()
                tr(yps.ap[:, 0:128], X.ap[:, 0, :], [X], [yps])
                u_ = U[0]
                V("pool", "tensor_tensor", [X, ident], [u_], out=u_.ap[:], in0=X.ap[:, 0, :], in1=ident.ap[:], op=ALU.add)
                yield
                V("act", "activation", [yps], [X], out=X.ap[:, 1, :], in_=yps.ap[:, 0:128], func=AF.Identity)
                yield
                for s in range(6):
                    Xn = XY[(s + 1) % 3]
                    xps = psum()
                    mm(xps.ap[:, 0:128], X.ap[:, 1, :], X.ap[:, 0, :], [X], [xps])
                    mm(xps.ap[:, 128:256], X.ap[:, 0, :], X.ap[:, 1, :], [X], [xps])
                    yield
                    V("act", "activation", [xps], [Xn], out=Xn.ap[:], in_=xps.ap[:, 0:256].rearrange("p (a b) -> p a b", a=2), func=AF.Identity)
                    ups = psum()
                    mm(ups.ap[:, 0:128], Xn.ap[:, 1, :], u_.ap[:], [Xn, u_], [ups])
                    yield
                    un = U[(s + 1) % 3]
                    V("dve", "tensor_tensor", [ups, u_], [un], out=un.ap[:], in0=ups.ap[:, 0:128], in1=u_.ap[:], op=ALU.add)
                    X = Xn; u_ = un
                cps = psum()
                mm(cps.ap[:, 0:128], tt.ap[:, 0, :], u_.ap[:], [tt, u_], [cps])
                yield
                V("act", "activation", [cps], [nk_], out=nk_.ap[:], in_=cps.ap[:, 0:128], func=AF.Copy, scale=-1.0)
                vps = psum()
                mm(vps.ap[:, 0:128], u_.ap[:], tt.ap[:, 2, :], [u_, tt], [vps], start=True, stop=False)
                mm(vps.ap[:, 0:128], nk_.ap[:], Sh.ap[:], [nk_, Sh], [vps], start=False, stop=True)
                yield
                V("act", "activation", [vps], [vn_], out=vn_.ap[:], in_=vps.ap[:, 0:128], func=AF.Identity)
                ops_ = psum()
                mm(ops_.ap[:, 0:128], qd_.ap[:], Sh.ap[:], [qd_, Sh], [ops_], start=True, stop=False)
                mm(ops_.ap[:, 0:128], qkT_.ap[:], vn_.ap[:], [qkT_, vn_], [ops_], start=False, stop=True)
                sps2 = psum()
                mm(sps2.ap[:, 0:128], tt.ap[:, 1, :], vn_.ap[:], [tt, vn_], [sps2])
                yield
                V("dve", "scalar_tensor_tensor", [Sh, tk, sps2], [Sh], out=Sh.ap[:], in0=Sh.ap[:], scalar=tk.ap[:, 5, hh:hh + 1], in1=sps2.ap[:, 0:128],
                  op0=ALU.mult, op1=ALU.add)
                V("act", "activation", [ops_], [rs_, so], out=rs_.ap[:], in_=ops_.ap[:, 0:128], func=AF.Square, accum_out=so.ap[:, 0:1])
                yield
                V("dve", "tensor_scalar", [so], [so], out=so.ap[:, 1:2], in0=so.ap[:, 0:1], scalar1=1.0 / 128, scalar2=EPS, op0=ALU.mult, op1=ALU.add)
                V("act", "activation", [so], [so], out=so.ap[:, 1:2], in_=so.ap[:, 1:2], func=AF.Sqrt)
                yield
                V("dve", "reciprocal", [so], [so], out=so.ap[:, 2:3], in_=so.ap[:, 1:2])
                V("dve", "scalar_tensor_tensor", [ops_, so, sg_], [rs_], out=rs_.ap[:], in0=ops_.ap[:, 0:128], scalar=so.ap[:, 2:3], in1=sg_.ap[:], op0=ALU.mult, op1=ALU.mult)
                P.dma(o_d.ap[rows, hh * 128:(hh + 1) * 128], rs_.ap[:], r=[rs_], w=[o_d])

            for t in range(NT):
                rows = slice(t * 128, (t + 1) * 128)
                ba = bas[t % 2]; tk = tks[t % 2]; rowsT = rowsTs[t % 2]
                pa_, pb_ = pc(rows, A_BETA, A_BETA + 32)
                P.dma(ba.ap[:], pa_, r=[pb_], w=[ba])
                V("act", "activation", [ba], [tk], out=tk.ap[:, 0, :], in_=ba.ap[:, 0:16], func=AF.Sigmoid)
                V("dve", "tensor_tensor", [ba, dtb], [tk], out=tk.ap[:, 7, :], in0=ba.ap[:, 16:32], in1=dtb.ap[:], op=ALU.add)
                V("act", "activation", [tk], [tk], out=tk.ap[:, 7, :], in_=tk.ap[:, 7, :], func=AF.Exp)
                V("act", "activation", [tk], [tk], out=tk.ap[:, 7, :], in_=tk.ap[:, 7, :], func=AF.Ln, bias=1.0)
                V("dve", "tensor_tensor", [tk, nea], [tk], out=tk.ap[:, 1, :], in0=tk.ap[:, 7, :], in1=nea.ap[:], op=ALU.mult)
                gps = slots[0]["psum"]()
                mm(gps.ap[:, 0:16], ltri.ap[:], tk.ap[:, 1, :], [ltri, tk], [gps])
                mm(gps.ap[:, 16:32], ones.ap[:], tk.ap[:, 1, :], [ones, tk], [gps])
                V("dve", "tensor_copy", [gps], [tk], out=tk.ap[:, 2, :], in_=gps.ap[:, 0:16])
                V("act", "activation", [gps], [tk], out=tk.ap[:, 3, :], in_=gps.ap[:, 0:16], func=AF.Exp)
                V("dve", "tensor_tensor", [gps, tk], [tk], out=tk.ap[:, 4, :], in0=gps.ap[:, 16:32], in1=tk.ap[:, 2, :], op=ALU.subtract)
                V("act", "activation", [tk], [tk], out=tk.ap[:, 4, :], in_=tk.ap[:, 4, :], func=AF.Exp)
                V("act", "activation", [gps], [tk], out=tk.ap[:, 5, :], in_=gps.ap[:, 16:32], func=AF.Exp)
                V("dve", "tensor_tensor", [tk], [tk], out=tk.ap[:, 6, :], in0=tk.ap[:, 0, :], in1=tk.ap[:, 3, :], op=ALU.mult)
                rps = slots[0]["psum"]()
                tr(rps.ap[0:16, 0:128], tk.ap[:, 2, :], [tk], [rps])
                tr(rps.ap[0:16, 128:256], tk.ap[:, 0, :], [tk], [rps])
                V("act", "activation", [rps], [rowsT], out=rowsT.ap[:], in_=rps.ap[0:16, 0:256], func=AF.Identity)
                for h0 in range(0, 16, NS):
                    gens = [head(t, h0 + i, slots[i], tk, rowsT, rows) for i in range(NS)]
                    live = list(gens)
                    while live:
                        nxt = []
                        for g_ in live:
                            try:
                                next(g_)
                                nxt.append(g_)
                            except StopIteration:
                                pass
                        live = nxt
        run_stage(s3a, "a")

        def s4a(sb):
            ots = [sb("ot%d" % i, [128, 5120]) for i in range(2)]
            oss = [sb("os%d" % i, [128, 40, 128], BF16) for i in range(2)]
            for t in range(NT):
                ot = ots[t % 2]; os_ = oss[t % 2]
                P.dma(ot.ap[:], o_d.ap[t * 128:(t + 1) * 128, :], r=[o_d], w=[ot])
                for c4 in range(10):
                    pt = psum()
                    for i in range(4):
                        c = c4 * 4 + i
                        tr(pt.ap[:, i * 128:(i + 1) * 128], ot.ap[:, c * 128:(c + 1) * 128], [ot], [pt])
                    if c4 % 2 == 0:
                        V("act", "activation", [pt], [os_], out=os_.ap[:, c4 * 4:c4 * 4 + 4, :], in_=pt.ap[:].rearrange("p (a b) -> p a b", a=4), func=AF.Identity)
                    else:
                        V("dve", "tensor_copy", [pt], [os_], out=os_.ap[:, c4 * 4:c4 * 4 + 4, :], in_=pt.ap[:].rearrange("p (a b) -> p a b", a=4))
                P.dma(oT_d.ap[t], os_.ap[:], r=[os_], w=[oT_d])
        run_stage(s4a, "t")

        def s4b(sb):
            wts = [sb("wbt%d" % i, [128, 40, 512], BF16) for i in range(2)]
            ats = [sb("abt%d" % i, [128, 40, 128], BF16) for i in range(4)]
            gts = [sb("gts%d" % i, [128, 3, 512]) for i in range(3)]
            ysb = [sb("ysb%d" % i, [128, 512]) for i in range(2)]
            ysT = [sb("ysT%d" % i, [128, 4, 128], BF16) for i in range(2)]
            wvv = Wl["w_br"].rearrange("(c p) n -> p c n", p=128)
            branches = ((0, 16), (16, 8), (24, 16))
            work = [(nb, t) for nb in range(8) for t in range(NT)]
            PF = 2

            def load_w(nb):
                wt = wts[nb % 2]
                for k4 in range(0, 40, 8):
                    P.dma(wt.ap[:, k4:k4 + 8, :], wvv[:, k4:k4 + 8, nb * 512:(nb + 1) * 512], w=[wt], q="pool")

            def load_a(i):
                nb, t = work[i]
                at = ats[i % 4]; g = gts[i % 3]
                rows = slice(t * 128, (t + 1) * 128)
                P.dma(at.ap[:], oT_d.ap[t], r=[oT_d], w=[at])
                for bi in range(3):
                    c0 = MERGE + bi * D + nb * 512
                    pa_, pb_ = pc(rows, c0, c0 + 512)
                    P.dma(g.ap[:, bi, :], pa_, r=[pb_], w=[g])

            load_w(0)
            for i in range(min(PF, len(work))):
                load_a(i)
            for i, (nb, t) in enumerate(work):
                wt = wts[nb % 2]
                if t == 0 and nb + 1 < 8:
                    load_w(nb + 1)
                if i + PF < len(work):
                    load_a(i + PF)
                at = ats[i % 4]; g = gts[i % 3]; y = ysb[i % 2]; yt = ysT[i % 2]
                V("act", "activation", [g], [g], out=g.ap[:], in_=g.ap[:], func=AF.Sigmoid)
                pss = []
                for bi, (k0, kn) in enumerate(branches):
                    ps = psum(); pss.append(ps)
                    for k in range(kn):
                        mm(ps.ap[:], at.ap[:, k0 + k, :], wt.ap[:, k0 + k, :], [at, wt], [ps], start=(k == 0), stop=(k == kn - 1))
                V("dve", "tensor_tensor", [pss[0], g], [y], out=y.ap[:], in0=pss[0].ap[:], in1=g.ap[:, 0, :], op=ALU.mult)
                V("dve", "tensor_tensor", [pss[1], g], [g], out=g.ap[:, 1, :], in0=pss[1].ap[:], in1=g.ap[:, 1, :], op=ALU.mult)
                V("dve", "tensor_tensor", [pss[2], g], [g], out=g.ap[:, 2, :], in0=pss[2].ap[:], in1=g.ap[:, 2, :], op=ALU.mult)
                V("pool", "tensor_tensor", [y, g], [y], out=y.ap[:], in0=y.ap[:], in1=g.ap[:, 1, :], op=ALU.add)
                V("pool", "tensor_tensor", [y, g], [y], out=y.ap[:], in0=y.ap[:], in1=g.ap[:, 2, :], op=ALU.add)
                pt = psum()
                for j in range(4):
                    tr(pt.ap[:, j * 128:(j + 1) * 128], y.ap[:, j * 128:(j + 1) * 128], [y], [pt])
                V("act", "activation", [pt], [yt], out=yt.ap[:], in_=pt.ap[:].rearrange("p (a b) -> p a b", a=4), func=AF.Identity)
                P.dma(yT_d.ap[t][:, nb * 4:nb * 4 + 4, :], yt.ap[:], r=[yt], w=[yT_d])
        run_stage(s4b, "m")

        if dbg is None or "o" in dbg:
            gemm_stage(yT_d, KC, Wl["w_out"], D, ob_d, "o")

        def s4d(sb):
            obv = ob_d.ap.rearrange("(n p) d -> n p d", p=128)
            nxv = nxt_x.ap.rearrange("(n p) d -> n p d", p=128)
            ots = [sb("fo%d" % i, [128, D]) for i in range(2)]
            xts = [sb("fx%d" % i, [128, D]) for i in range(2)]
            junk = sb("fjunk", [128, D])
            for t in range(NT):
                ot = ots[t % 2]; xt = xts[t % 2]
                P.dma(ot.ap[:], obv[t], r=[ob_d], w=[ot])
                P.dma(xt.ap[:], xv[t], r=[cur_x], w=[xt])
                V("act", "activation", [ot], [junk, ss], out=junk.ap[:], in_=ot.ap[:], func=AF.Square, accum_out=ss.ap[:, 0:1])
                rsqrt_to(rs, ss, 1.0 / D, 1)
                V("dve", "scalar_tensor_tensor", [ot, rs, ggB], [ot], out=ot.ap[:], in0=ot.ap[:], scalar=rs.ap[:, 0:1], in1=ggB.ap[:], op0=ALU.mult, op1=ALU.mult)
                V("pool", "tensor_tensor", [ot, xt], [ot], out=ot.ap[:], in0=ot.ap[:], in1=xt.ap[:], op=ALU.add)
                P.dma(nxv[t], ot.ap[:], r=[ot], w=[nxt_x])
        run_stage(s4d, "f")
        cur_x = nxt_x

    stack.close()
    return nc


def _consts():
    ident = np.eye(128, dtype=np.float32)
    s = np.arange(128)
    ltri = (s[:, None] <= s[None, :]).astype(np.float32)
    lstr = (s[:, None] < s[None, :]).astype(np.float32)
    sel = np.zeros((16, 16, 128), np.float32)
    for h in range(16):
        sel[h, h, :] = 1.0
    slopes = 2.0 ** (-8.0 * np.arange(1, 25, dtype=np.float32) / 24)
    i = np.arange(128)[:, None]
    j = np.arange(256)[None, :]
    rel = i + 128 - j
    valid = (rel >= 0) & (rel <= 128)
    bm = np.zeros((24, 128, 256), np.float32)
    for h in range(24):
        d = (1, 4, 16)[h // 8]
        bm[h] = np.where(valid, -slopes[h] * (d * rel).astype(np.float32), NEG)
    return dict(ident=ident, ltri=ltri, lstr=lstr, sel=sel.reshape(16, 2048), bmask=bm)


def _layer_inputs(inp, l):
    f = np.float32
    def fm(v, n):
        return np.ascontiguousarray(v.reshape(n, 128).T).astype(f)
    def bc(v):
        return np.ascontiguousarray(np.broadcast_to(v[None, :], (128, v.shape[0]))).astype(f)
    b_ada = inp["b_ada"][l]
    conv = inp["conv_a"][l]
    convT = np.ascontiguousarray(conv.reshape(4, 48, 128).transpose(2, 1, 0))
    w_br = np.concatenate([inp["w_br_a"][l], inp["w_br_b"][l], inp["w_br_c"][l]], axis=0)
    return {
        "w_ada%d" % l: inp["w_ada"][l], "b_adaT%d" % l: fm(b_ada[0:8192], 64), "b_gateB%d" % l: bc(b_ada[8192:]),
        "g_preT%d" % l: fm(inp["g_pre"][l], 32), "g_postB%d" % l: bc(inp["g_post"][l]),
        "w_in%d" % l: inp["w_in"][l], "convT%d" % l: convT,
        "a_logB%d" % l: bc(inp["a_log"][l]), "dt_biasB%d" % l: bc(inp["dt_bias"][l]),
        "g_onormB%d" % l: bc(inp["g_onorm_a"][l]),
        "ln_gB%d" % l: bc(inp["ln_c_g"][l]), "ln_bB%d" % l: bc(inp["ln_c_b"][l]),
        "w_spT%d" % l: np.ascontiguousarray(inp["w_spatial"][l].transpose(2, 0, 1)),
        "b_spT%d" % l: np.ascontiguousarray(inp["b_spatial"][l].T),
        "w_br%d" % l: w_br, "w_out%d" % l: inp["w_out"][l],
    }


def kernel(**inputs):
    inp = {k: np.asarray(v) for k, v in inputs.items()}
    x = inp["x"]
    B, T, _ = x.shape
    L = inp["w_in"].shape[0]
    nc = build(T, L)
    shared = _consts()
    for l in range(L):
        shared.update(_layer_inputs(inp, l))
    in_maps = []
    for b in range(B):
        m = dict(shared)
        m["x"] = np.ascontiguousarray(x[b])
        m["cT"] = np.ascontiguousarray(inp["c"][b].reshape(KC, 128).T)
        in_maps.append(m)
    res = run_bass_kernel_spmd(nc, in_maps, core_ids=list(range(B)))
    return np.stack([res.results[b]["out"] for b in range(B)], axis=0).astype(np.float32)
```

```python
import numpy as np
import concourse.bass as bass
import concourse.mybir as mybir
from concourse.bass_utils import run_bass_kernel_spmd

F32 = mybir.dt.float32
BF16 = mybir.dt.bfloat16
AF = mybir.ActivationFunctionType
ALU = mybir.AluOpType
AX = mybir.AxisListType

D = 4096
KC = 32
NW = 36896
A_QKV, A_BETA, A_ALPHA, A_GATE = 0, 6144, 6160, 6176
B_QKV, B_GATE = 8224, 17440
C_U, C_V, C_GATE = 18464, 20512, 22560
MERGE = 24608
EPS = 1e-6
NEG = -30000.0


class Buf:
    def __init__(self, ap=None):
        self.ap = ap
        self.w = {}
        self.r = {}


class Prog:
    ENG = ("pe", "act", "dve", "pool", "sp")

    def __init__(self, nc, stack):
        self.nc = nc
        self.q = {e: [] for e in self.ENG}
        self.cnt = {e: 0 for e in self.ENG}
        self.seen = {e: {} for e in self.ENG}
        self.esem = {e: stack.enter_context(nc.semaphore("s_" + e)) for e in self.ENG if e != "sp"}
        self.dsem = {}
        self.dcnt = {}
        self.dnext = {}
        for qn, k in (("sp", 8), ("pool", 4), ("act", 4)):
            self.dsem[qn] = [stack.enter_context(nc.semaphore("d_%s%d" % (qn, i))) for i in range(k)]
            self.dcnt[qn] = [0] * k
            self.dnext[qn] = 0
        self.ninst = 0

    def _waits(self, eng, deps):
        for key, val in deps.items():
            if self.seen[eng].get(key, 0) < val:
                self.seen[eng][key] = val
                sem = key[1]
                self.q[eng].append(lambda e, sem=sem, val=val: e.wait_ge(sem, val))

    def _deps(self, eng, r, w):
        deps = {}
        for b in r:
            for k, v in b.w.items():
                deps[k] = max(deps.get(k, 0), v)
            if getattr(b, "psum", False):
                for k, v in b.r.items():
                    if k[0] != "e_" + eng:
                        deps[k] = max(deps.get(k, 0), v)
        for b in w:
            for dct in (b.w, b.r):
                for k, v in dct.items():
                    deps[k] = max(deps.get(k, 0), v)
        if eng == "pe":
            deps = {k: v for k, v in deps.items() if k[0] != "e_pe"}
        return deps

    def _mark(self, tok, r, w):
        k, v = tok
        for b in r:
            b.r[k] = max(b.r.get(k, 0), v)
        for b in w:
            b.w[k] = max(b.w.get(k, 0), v)

    def op(self, eng, fn, r=(), w=(), sig=True):
        self.ninst += 1
        self._waits(eng, self._deps(eng, r, w))
        sem = self.esem[eng]
        if sig:
            self.cnt[eng] += 1
            self.q[eng].append(lambda e, fn=fn, sem=sem: fn(e).then_inc(sem, 1))
            tok = (("e_" + eng, sem), self.cnt[eng])
        else:
            self.q[eng].append(lambda e, fn=fn: fn(e))
            tok = (("e_" + eng, sem), self.cnt[eng] + 1)
        self._mark(tok, r, w)

    def dma(self, out, in_, r=(), w=(), q="sp", **kw):
        self.ninst += 1
        i = self.dnext[q]
        self.dnext[q] = (i + 1) % len(self.dsem[q])
        sem = self.dsem[q][i]
        key = ("d_%s%d" % (q, i), sem)
        deps = self._deps(q, r, w)
        if self.dcnt[q][i] > 0:
            deps[key] = max(deps.get(key, 0), 16 * self.dcnt[q][i])
        self._waits(q, deps)
        self.dcnt[q][i] += 1
        self.q[q].append(lambda e, out=out, in_=in_, sem=sem, kw=kw: e.dma_start(out=out, in_=in_, **kw).then_inc(sem, 16))
        self._mark((key, 16 * self.dcnt[q][i]), r, w)

    def barrier(self):
        deps = {}
        for e, s in self.esem.items():
            if self.cnt[e] > 0:
                deps[("e_" + e, s)] = self.cnt[e]
        for qn in self.dsem:
            for i, s in enumerate(self.dsem[qn]):
                if self.dcnt[qn][i] > 0:
                    deps[("d_%s%d" % (qn, i), s)] = 16 * self.dcnt[qn][i]
        for e in self.ENG:
            self._waits(e, dict(deps))

    def clear(self):
        self.q = {e: [] for e in self.ENG}

    def finish(self, bufs):
        deps = {}
        for b in bufs:
            for k, v in b.w.items():
                deps[k] = max(deps.get(k, 0), v)
        self._waits("sp", deps)

    def emit(self, block):
        nc = self.nc

        @block.tensor
        def _(e):
            for f in self.q["pe"]:
                f(e)

        @block.scalar
        def _(e):
            for f in self.q["act"]:
                f(e)

        @block.vector
        def _(e):
            for f in self.q["dve"]:
                f(e)

        @block.gpsimd
        def _(e):
            for f in self.q["pool"]:
                f(e)

        @block.sync
        def _(e):
            for f in self.q["sp"]:
                f(e)


def build(T, L, dbg=None):
    from contextlib import ExitStack
    nc = bass.Bass("TRN2", target_bir_lowering=False)
    NT = T // 128
    stack = ExitStack()
    uid = [0]

    def din(name, shape, dt=F32):
        return nc.dram_tensor(name, list(shape), dt, kind="ExternalInput").ap()

    def dscr(name, shape, dt=F32, kind=None):
        if kind is None:
            return Buf(nc.dram_tensor(name, list(shape), dt).ap())
        return Buf(nc.dram_tensor(name, list(shape), dt, kind=kind).ap())

    class Lazy(dict):
        def __init__(self, shapes):
            dict.__init__(self)
            self.shapes = shapes

        def __missing__(self, k):
            nm, shp = self.shapes[k]
            self[k] = din(nm, shp)
            return self[k]

    x_in = din("x", [T, D]) if (dbg is None or "1" in dbg or "f" in dbg) else None
    cT_in = din("cT", [128, KC])
    out_ap = nc.dram_tensor("out", [T, D], F32, kind="ExternalOutput").ap()
    ident_in = din("ident", [128, 128])
    ltri_in = din("ltri", [128, 128])
    lstr_in = din("lstr", [128, 128])
    sel_in = din("sel", [16, 16 * 128])
    bmask_in = din("bmask", [24, 128, 256])
    W = []
    for l in range(L):
        W.append(Lazy(dict(
            w_ada=("w_ada%d" % l, [D, 3 * D]), b_adaT=("b_adaT%d" % l, [128, 64]),
            b_gateB=("b_gateB%d" % l, [128, D]),
            g_preT=("g_preT%d" % l, [128, KC]), g_postB=("g_postB%d" % l, [128, D]),
            w_in=("w_in%d" % l, [D, NW]), convT=("convT%d" % l, [128, 48, 4]),
            a_logB=("a_logB%d" % l, [128, 16]), dt_biasB=("dt_biasB%d" % l, [128, 16]),
            g_onormB=("g_onormB%d" % l, [128, 128]),
            ln_gB=("ln_gB%d" % l, [128, 2048]), ln_bB=("ln_bB%d" % l, [128, 2048]),
            w_spT=("w_spT%d" % l, [128, 16, 128]), b_spT=("b_spT%d" % l, [128, 16]),
            w_br=("w_br%d" % l, [5120, D]), w_out=("w_out%d" % l, [D, D]),
        )))
    hT_d = dscr("hT_d", [NT, 128, KC, 128], BF16, kind=("ExternalInput" if (dbg is not None and "1" not in dbg and "g" in dbg) else None))
    REG = [(0, 6144), (6144, 8224), (8224, 11296), (11296, 14368), (14368, 17440), (17440, 18464),
           (18464, 24608), (24608, 28704), (28704, 32800), (32800, 36896)]
    p_in = "ExternalInput" if (dbg is not None and "g" not in dbg) else None
    P_regs = [[a, b, None, i] for i, (a, b) in enumerate(REG)]

    def preg(e):
        if e[2] is None:
            e[2] = dscr("P_d%d" % e[3], [T, e[1] - e[0]], kind=p_in)
        return e[2]

    def pc(rows, c0, c1):
        for e in P_regs:
            if e[0] <= c0 and c1 <= e[1]:
                buf = preg(e)
                return buf.ap[rows, c0 - e[0]:c1 - e[0]], buf
        raise AssertionError((c0, c1))

    def p_store(t, n0, nw, s):
        rows = slice(t * 128, (t + 1) * 128)
        for e in P_regs:
            a, b = e[0], e[1]
            lo = max(a, n0); hi = min(b, n0 + nw)
            if lo < hi:
                buf = preg(e)
                P.dma(buf.ap[rows, lo - a:hi - a], s.ap[:, lo - n0:hi - n0], r=[s], w=[buf])
    o_kind = None
    if dbg is not None:
        o_kind = "ExternalOutput" if any(c in dbg for c in "abc") else "ExternalInput"
    o_d = dscr("o_d", [T, 5120], kind=o_kind)
    og_d = dscr("og_d", [3, T, 8, 132], kind=("ExternalOutput" if (dbg is not None and "b" in dbg) else None))
    oT_d = dscr("oT_d", [NT, 128, 40, 128], BF16)
    yT_d = dscr("yT_d", [NT, 128, KC, 128], BF16)
    ob_d = dscr("ob_d", [T, D])
    x1_d = dscr("x1_d", [T, D])
    xin_b = Buf(x_in)
    if dbg is not None and "f" not in dbg:
        pass
    out_b = Buf(out_ap)

    P = Prog(nc, stack)

    def gsb(name, shape, dt=F32):
        return Buf(stack.enter_context(nc.sbuf_tensor("g_" + name, list(shape), dt)))

    psb = [Buf(stack.enter_context(nc.psum_tensor("ps%d" % i, [128, 512], F32))) for i in range(8)]
    for b_ in psb:
        b_.psum = True
    pcount = [0]

    def psum():
        pcount[0] += 1
        return psb[pcount[0] % 8]

    import os as _os
    kstop = int(_os.environ.get("KSTOP", "99"))
    nstage = [0]

    def run_stage(fn, letter="k"):
        nstage[0] += 1
        if nstage[0] > kstop:
            return
        if dbg is not None and letter != "k" and letter not in dbg:
            return
        with ExitStack() as st:
            def sb(name, shape, dt=F32):
                uid[0] += 1
                return Buf(st.enter_context(nc.sbuf_tensor("t_%s_%d" % (name, uid[0]), list(shape), dt)))
            fn(sb)
            P.barrier()
            with nc.Block() as block:
                P.emit(block)
            P.clear()

    ident = gsb("ident", [128, 128]); ltri = gsb("ltri", [128, 128]); lstr = gsb("lstr", [128, 128])
    sel = gsb("sel", [16, 2048]); ones = gsb("ones", [128, 128])
    modT = gsb("modT", [128, 64]); gmod = gsb("gmod", [128, KC]); ggB = gsb("ggB", [128, D])
    condT = gsb("condT", [128, KC])
    ss = gsb("ss", [128, 4]); rs = gsb("rs", [128, 4])

    def rsqrt_to(dst, src, scale, n):
        P.op("dve", lambda e: e.tensor_scalar(out=dst.ap[:, 0:n], in0=src.ap[:, 0:n], scalar1=scale, scalar2=EPS,
                                              op0=ALU.mult, op1=ALU.add), r=[src], w=[dst])
        P.op("act", lambda e: e.activation(out=dst.ap[:, 0:n], in_=dst.ap[:, 0:n], func=AF.Sqrt), r=[dst], w=[dst])
        P.op("dve", lambda e: e.reciprocal(out=dst.ap[:, 0:n], in_=dst.ap[:, 0:n]), r=[dst], w=[dst])

    def mm(out, lhsT, rhs, r, w, start=True, stop=True, sig=None):
        P.op("pe", lambda e: e.matmul(out, lhsT=lhsT, rhs=rhs, start=start, stop=stop), r=r, w=w,
             sig=stop if sig is None else sig)

    def tr(out, in_, r, w):
        P.op("pe", lambda e: e.transpose(out, in_, ident.ap[:]), r=list(r) + [ident], w=w)

    def V(eng, fname, r, w, **kw):
        P.op(eng, lambda e: getattr(e, fname)(**kw), r=r, w=w)

    def s_const(sb):
        cT = sb("cT", [128, KC])
        P.dma(ident.ap[:], ident_in[:, :], w=[ident])
        P.dma(ltri.ap[:], ltri_in[:, :], w=[ltri])
        P.dma(lstr.ap[:], lstr_in[:, :], w=[lstr])
        P.dma(sel.ap[:], sel_in[:, :], w=[sel])
        V("pool", "memset", [], [ones], ap=ones.ap[:], constant=1.0)
        P.dma(cT.ap[:], cT_in[:, :], w=[cT])
        V("act", "activation", [cT], [condT], out=condT.ap[:], in_=cT.ap[:], func=AF.Silu)
    run_stage(s_const)

    cur_x = xin_b
    for l in range(L):
        Wl = W[l]
        nxt_x = out_b if l == L - 1 else x1_d
        xv = cur_x.ap.rearrange("(n p) d -> n p d", p=128) if cur_x.ap is not None else None

        def s0(sb):
            wa = [sb("wa%d" % i, [128, KC, 128]) for i in range(2)]
            condB = sb("condB", [128, KC, 128])
            b_adaT = sb("b_adaT", [128, 64]); g_preT = sb("g_preT", [128, KC]); gpost = sb("gpost", [128, D])
            for k in range(KC):
                V("dve", "tensor_scalar", [ones, condT], [condB], out=condB.ap[:, k, :], in0=ones.ap[:], scalar1=condT.ap[:, k:k + 1],
                  scalar2=None, op0=ALU.mult)
            P.dma(b_adaT.ap[:], Wl["b_adaT"][:, :], w=[b_adaT])
            P.dma(g_preT.ap[:], Wl["g_preT"][:, :], w=[g_preT])
            P.dma(ggB.ap[:], Wl["b_gateB"][:, :], w=[ggB])
            P.dma(gpost.ap[:], Wl["g_postB"][:, :], w=[gpost])
            wv = Wl["w_ada"].rearrange("(c p) n -> p c n", p=128)
            mps = psum()
            for j in range(96):
                wt = wa[j % 2]
                P.dma(wt.ap[:], wv[:, :, j * 128:(j + 1) * 128], w=[wt])
                if j < 64:
                    for k in range(KC):
                        mm(mps.ap[:, j:j + 1], wt.ap[:, k, :], condT.ap[:, k:k + 1], [wt, condT], [mps], start=(k == 0), stop=(k == KC - 1))
                else:
                    gp = psum()
                    if gp is mps:
                        gp = psum()
                    for k in range(KC):
                        mm(gp.ap[:, 0:128], condB.ap[:, k, :], wt.ap[:, k, :], [wt, condB], [gp], start=(k == 0), stop=(k == KC - 1))
                    c0 = (j - 64) * 128
                    V("dve", "tensor_tensor", [gp, ggB], [ggB], out=ggB.ap[:, c0:c0 + 128], in0=gp.ap[:, 0:128], in1=ggB.ap[:, c0:c0 + 128], op=ALU.add)
            V("dve", "tensor_tensor", [mps, b_adaT], [modT], out=modT.ap[:], in0=mps.ap[:, 0:64], in1=b_adaT.ap[:], op=ALU.add)
            V("dve", "scalar_tensor_tensor", [modT, g_preT], [gmod], out=gmod.ap[:], in0=modT.ap[:, 32:64], scalar=1.0, in1=g_preT.ap[:],
              op0=ALU.add, op1=ALU.mult)
            V("dve", "tensor_tensor", [ggB, gpost], [ggB], out=ggB.ap[:], in0=ggB.ap[:], in1=gpost.ap[:], op=ALU.mult)
        run_stage(s0, "0")

        def s1(sb):
            xts = [sb("xt%d" % i, [128, D]) for i in range(2)]
            junk = sb("junk", [128, D])
            hss = [sb("hs%d" % i, [128, KC, 128], BF16) for i in range(2)]
            def ld1(t):
                P.dma(xts[t % 2].ap[:], xv[t], r=[cur_x], w=[xts[t % 2]])
            ld1(0)
            for t in range(NT):
                xt = xts[t % 2]; hs = hss[t % 2]
                if t + 1 < NT:
                    ld1(t + 1)
                V("act", "activation", [xt], [junk, ss], out=junk.ap[:], in_=xt.ap[:], func=AF.Square, accum_out=ss.ap[:, 0:1])
                rsqrt_to(rs, ss, 1.0 / D, 1)
                V("dve", "tensor_scalar", [xt, rs], [xt], out=xt.ap[:], in0=xt.ap[:], scalar1=rs.ap[:, 0:1], scalar2=None, op0=ALU.mult)
                for c4 in range(8):
                    pt = psum()
                    for i in range(4):
                        c = c4 * 4 + i
                        tr(pt.ap[:, i * 128:(i + 1) * 128], xt.ap[:, c * 128:(c + 1) * 128], [xt], [pt])
                    for i in range(4):
                        c = c4 * 4 + i
                        if i % 2 == 0:
                            V("act", "activation", [pt, gmod, modT], [hs], out=hs.ap[:, c, :], in_=pt.ap[:, i * 128:(i + 1) * 128], func=AF.Identity,
                              scale=gmod.ap[:, c:c + 1], bias=modT.ap[:, c:c + 1])
                        else:
                            V("dve", "tensor_scalar", [pt, gmod, modT], [hs], out=hs.ap[:, c, :], in0=pt.ap[:, i * 128:(i + 1) * 128],
                              scalar1=gmod.ap[:, c:c + 1], scalar2=modT.ap[:, c:c + 1], op0=ALU.mult, op1=ALU.add)
                P.dma(hT_d.ap[t], hs.ap[:], r=[hs], w=[hT_d])
        run_stage(s1, "1")

        def gemm_stage(a_d, kc, wd, n_total, dst, letter):
            def body(sb):
                wts = [sb("wt%d" % i, [128, kc, 512], BF16) for i in range(2)]
                ats = [sb("at%d" % i, [128, kc, 128], BF16) for i in range(4)]
                stg = [sb("stg%d" % i, [128, 512]) for i in range(4)]
                wvv = wd.rearrange("(c p) n -> p c n", p=128)
                nblk = (n_total + 511) // 512
                PF = 3
                work = [(nb, t) for nb in range(nblk) for t in range(NT)]

                def load_w(nb):
                    n0 = nb * 512
                    nw = min(512, n_total - n0)
                    wt = wts[nb % 2]
                    for k4 in range(0, kc, 8):
                        P.dma(wt.ap[:, k4:k4 + 8, 0:nw], wvv[:, k4:k4 + 8, n0:n0 + nw], w=[wt], q="pool")

                def load_a(i):
                    nb, t = work[i]
                    at = ats[i % 4]
                    P.dma(at.ap[:], a_d.ap[t], r=[a_d], w=[at])

                load_w(0)
                for i in range(min(PF, len(work))):
                    load_a(i)
                ec = 0
                for i, (nb, t) in enumerate(work):
                    n0 = nb * 512
                    nw = min(512, n_total - n0)
                    wt = wts[nb % 2]
                    if t == 0 and nb + 1 < nblk:
                        load_w(nb + 1)
                    if i + PF < len(work):
                        load_a(i + PF)
                    at = ats[i % 4]
                    ps = psum()
                    for k in range(kc):
                        mm(ps.ap[:, 0:nw], at.ap[:, k, :], wt.ap[:, k, 0:nw], [at, wt], [ps], start=(k == 0), stop=(k == kc - 1))
                    s = stg[ec % 4]
                    if ec % 2 == 0:
                        V("act", "activation", [ps], [s], out=s.ap[:, 0:nw], in_=ps.ap[:, 0:nw], func=AF.Identity)
                    else:
                        V("dve", "tensor_copy", [ps], [s], out=s.ap[:, 0:nw], in_=ps.ap[:, 0:nw])
                    ec += 1
                    if dst is None:
                        p_store(t, n0, nw, s)
                    else:
                        P.dma(dst.ap[t * 128:(t + 1) * 128, n0:n0 + nw], s.ap[:, 0:nw], r=[s], w=[dst])
            run_stage(body, letter)

        if dbg is None or "g" in dbg:
            gemm_stage(hT_d, KC, Wl["w_in"], NW, None, "g")

        def s3c(sb):
            lng = sb("lng", [128, 2048]); lnb = sb("lnb", [128, 2048])
            wsp = sb("wsp", [128, 16, 128]); wspb = sb("wspb", [128, 16, 128], BF16); bsp = sb("bsp", [128, 16])
            P.dma(lng.ap[:], Wl["ln_gB"][:, :], w=[lng]); P.dma(lnb.ap[:], Wl["ln_bB"][:, :], w=[lnb])
            P.dma(wsp.ap[:], Wl["w_spT"][:, :, :], w=[wsp]); P.dma(bsp.ap[:], Wl["b_spT"][:, :], w=[bsp])
            for g in range(16):
                V("dve", "tensor_tensor", [wsp, ltri], [wspb], out=wspb.ap[:, g, :], in0=wsp.ap[:, g, :], in1=ltri.ap[:], op=ALU.mult)
            uvg = [sb("uvg%d" % i, [128, 3, 2048]) for i in range(2)]
            tmp = sb("tmpc", [128, 2, 2048]); vnb = sb("vnb", [128, 2048], BF16)
            st6 = sb("st6", [128, 4, 6]); mv = sb("mv", [128, 4]); orow = [sb("orow%d" % i, [128, 2048]) for i in range(2)]

            def gelu(xa, ta, eng2):
                V("pool", "tensor_tensor", [uv], [tmp], out=ta, in0=xa, in1=xa, op=ALU.mult)
                V(eng2, "tensor_scalar", [tmp], [tmp], out=ta, in0=ta, scalar1=0.044715, scalar2=1.0, op0=ALU.mult, op1=ALU.add)
                V("pool", "tensor_tensor", [uv, tmp], [tmp], out=ta, in0=ta, in1=xa, op=ALU.mult)
                V("act", "activation", [tmp], [tmp], out=ta, in_=ta, func=AF.Sigmoid, scale=1.5957691216)
                V(eng2, "tensor_tensor", [uv, tmp], [uv], out=xa, in0=xa, in1=ta, op=ALU.mult)
            def ldc(t):
                pa_, pb_ = pc(slice(t * 128, (t + 1) * 128), C_U, C_U + 6144)
                P.dma(uvg[t % 2].ap[:], pa_.rearrange("p (a b) -> p a b", a=3), r=[pb_], w=[uvg[t % 2]])
            ldc(0)
            for t in range(NT):
                uv = uvg[t % 2]; orw = orow[t % 2]
                rows = slice(t * 128, (t + 1) * 128)
                if t + 1 < NT:
                    ldc(t + 1)
                gelu(uv.ap[:, 0, :], tmp.ap[:, 0, :], "dve")
                gelu(uv.ap[:, 1, :], tmp.ap[:, 1, :], "dve")
                for i in range(4):
                    V("dve", "bn_stats", [uv], [st6], out=st6.ap[:, i, :], in_=uv.ap[:, 1, i * 512:(i + 1) * 512])
                V("dve", "bn_aggr", [st6], [mv], out=mv.ap[:, 0:2], in_=st6.ap[:].rearrange("p a b -> p (a b)"))
                rsqrt_to(rs, Buf_view(mv, 1), 1.0, 1)
                V("dve", "tensor_scalar", [uv, mv, rs], [uv], out=uv.ap[:, 1, :], in0=uv.ap[:, 1, :], scalar1=mv.ap[:, 0:1], scalar2=rs.ap[:, 0:1],
                  op0=ALU.subtract, op1=ALU.mult)
                V("pool", "tensor_tensor", [uv, lng], [uv], out=uv.ap[:, 1, :], in0=uv.ap[:, 1, :], in1=lng.ap[:], op=ALU.mult)
                V("dve", "tensor_tensor", [uv, lnb], [vnb], out=vnb.ap[:], in0=uv.ap[:, 1, :], in1=lnb.ap[:], op=ALU.add)
                V("act", "activation", [uv], [uv], out=uv.ap[:, 2, :], in_=uv.ap[:, 2, :], func=AF.Silu)
                V("pool", "tensor_tensor", [uv], [uv], out=uv.ap[:, 0, :], in0=uv.ap[:, 0, :], in1=uv.ap[:, 2, :], op=ALU.mult)
                for g4 in range(4):
                    ps = psum()
                    for i in range(4):
                        g = g4 * 4 + i
                        mm(ps.ap[:, i * 128:(i + 1) * 128], wspb.ap[:, g, :], vnb.ap[:, g * 128:(g + 1) * 128], [wspb, vnb], [ps])
                    for i in range(4):
                        g = g4 * 4 + i
                        V("dve", "scalar_tensor_tensor", [ps, bsp, uv], [orw], out=orw.ap[:, g * 128:(g + 1) * 128], in0=ps.ap[:, i * 128:(i + 1) * 128],
                          scalar=bsp.ap[:, g:g + 1], in1=uv.ap[:, 0, g * 128:(g + 1) * 128], op0=ALU.add, op1=ALU.mult)
                P.dma(o_d.ap[rows, 3072:5120], orw.ap[:], r=[orw], w=[o_d])

        def Buf_view(b, col):
            v = Buf(b.ap[:, col:col + 1]); v.w = b.w; v.r = b.r
            return v
        run_stage(s3c, "c")

        def s3b(sb):
            bm = sb("bm", [128, 24, 256])
            P.dma(bm.ap[:], bmask_in.rearrange("h p j -> p h j"), w=[bm])
            NSB = 4

            def mkalloc(b0, b1):
                st_ = [0]

                def al():
                    st_[0] += 1
                    return psb[b0] if st_[0] % 2 == 1 else psb[b1]
                return al
            bslots = []
            for i in range(NSB):
                bslots.append(dict(
                    psum=mkalloc(2 * i, 2 * i + 1),
                    a=[sb("qkv%d_%d" % (i, k), [128, 3, 128]) for k in range(2)],
                    kt=[sb("kT2_%d_%d" % (i, k), [128, 128], BF16) for k in range(2)],
                    vb=[sb("vbf%d_%d" % (i, k), [128, 128], BF16) for k in range(2)],
                    q_=sb("qTb%d" % i, [128, 128], BF16), s_=sb("sbt%d" % i, [128, 256]), p_=sb("pb%d" % i, [128, 256]),
                    pt_=sb("pT%d" % i, [128, 256], BF16), t_=sb("stb%d" % i, [128, 8]),
                    o_=[sb("oo%d_%d" % (i, k), [128, 132]) for k in range(2)]))
            kb = int(_os.environ.get("KB", "0"))

            def stream(gi, d, j, r, B):
                hidx = gi * 8 + j
                cq = B_QKV + hidx * 128
                nblk = T // (128 * d)
                psum_ = B["psum"]
                q_ = B["q_"]; s_ = B["s_"]; p_ = B["p_"]; pt_ = B["pt_"]; t_ = B["t_"]
                def ldb(n):
                    a = B["a"][n % 2]
                    t0 = n * 128 * d + r
                    for s3 in range(3):
                        pa_, pb_ = pc(slice(t0, t0 + 127 * d + 1, d), cq + s3 * 3072, cq + s3 * 3072 + 128)
                        P.dma(a.ap[:, s3, :], pa_, r=[pb_], w=[a])
                ldb(0)
                for n in range(nblk):
                    a = B["a"][n % 2]; kt = B["kt"][n % 2]; vb = B["vb"][n % 2]
                    kp = B["kt"][(n - 1) % 2]; vp = B["vb"][(n - 1) % 2]; o_ = B["o_"][n % 2]
                    t0 = n * 128 * d + r
                    ps = psum_()
                    tr(ps.ap[:, 0:128], a.ap[:, 0, :], [a], [ps])
                    tr(ps.ap[:, 128:256], a.ap[:, 1, :], [a], [ps])
                    yield
                    V("act", "activation", [ps], [q_], out=q_.ap[:], in_=ps.ap[:, 0:128], func=AF.Identity)
                    V("act", "activation", [ps], [kt], out=kt.ap[:], in_=ps.ap[:, 128:256], func=AF.Identity)
                    V("pool", "tensor_copy", [a], [vb], out=vb.ap[:], in_=a.ap[:, 2, :])
                    if n + 1 < nblk:
                        ldb(n + 1)
                    sp_ = psum_()
                    k0 = 0 if n > 0 else 128
                    if n > 0:
                        mm(sp_.ap[:, 0:128], q_.ap[:], kp.ap[:], [q_, kp], [sp_])
                    mm(sp_.ap[:, 128:256], q_.ap[:], kt.ap[:], [q_, kt], [sp_])
                    yield
                    V("dve", "scalar_tensor_tensor", [sp_, bm], [s_], out=s_.ap[:, k0:256], in0=sp_.ap[:, k0:256], scalar=128.0 ** -0.5,
                      in1=bm.ap[:, hidx, k0:256], op0=ALU.mult, op1=ALU.add)
                    V("dve", "reduce_max", [s_], [t_], out=t_.ap[:, 0:1], in_=s_.ap[:, k0:256], axis=AX.X)
                    V("dve", "tensor_scalar", [t_], [t_], out=t_.ap[:, 1:2], in0=t_.ap[:, 0:1], scalar1=-1.0, scalar2=None, op0=ALU.mult)
                    yield
                    V("act", "activation", [s_, t_], [p_, t_], out=p_.ap[:, k0:256], in_=s_.ap[:, k0:256], func=AF.Exp, bias=t_.ap[:, 1:2],
                      accum_out=t_.ap[:, 2:3])
                    tp = psum_()
                    if n > 0:
                        tr(tp.ap[:, 0:128], p_.ap[:, 0:128], [p_], [tp])
                    tr(tp.ap[:, 128:256], p_.ap[:, 128:256], [p_], [tp])
                    yield
                    V("act", "activation", [tp], [pt_], out=pt_.ap[:, k0:256], in_=tp.ap[:, k0:256], func=AF.Identity)
                    op_ = psum_()
                    if n > 0:
                        mm(op_.ap[:, 0:128], pt_.ap[:, 0:128], vp.ap[:], [pt_, vp], [op_], start=True, stop=False)
                    mm(op_.ap[:, 0:128], pt_.ap[:, 128:256], vb.ap[:], [pt_, vb], [op_], start=(n == 0), stop=True)
                    V("dve", "reciprocal", [t_], [t_], out=t_.ap[:, 3:4], in_=t_.ap[:, 2:3])
                    V("act", "activation", [t_], [t_], out=t_.ap[:, 4:5], in_=t_.ap[:, 2:3], func=AF.Ln)
                    yield
                    V("dve", "tensor_scalar", [op_, t_], [o_], out=o_.ap[:, 0:128], in0=op_.ap[:, 0:128], scalar1=t_.ap[:, 3:4], scalar2=None, op0=ALU.mult)
                    V("dve", "tensor_tensor", [t_], [o_], out=o_.ap[:, 128:129], in0=t_.ap[:, 4:5], in1=t_.ap[:, 0:1], op=ALU.add)
                    P.dma(og_d.ap[gi, t0:t0 + 127 * d + 1:d, j, :], o_.ap[:], r=[o_], w=[og_d])
                    yield

            streams = [(gi, d, j, r) for gi, d in enumerate((1, 4, 16)) for j in range(8) for r in range(d)]
            if kb == 1:
                streams = []
            sit = iter(streams)
            active = [None] * NSB
            done = False
            while True:
                for i in range(NSB):
                    if active[i] is None and not done:
                        try:
                            gi, d, j, r = next(sit)
                            active[i] = stream(gi, d, j, r, bslots[i])
                        except StopIteration:
                            done = True
                    if active[i] is not None:
                        try:
                            next(active[i])
                        except StopIteration:
                            active[i] = None
                if done and all(a_ is None for a_ in active):
                    break
            kb_skip = (kb == 2)
            ogt = [sb("ogt%d" % i, [128, 3, 8, 132]) for i in range(2)]
            gt = [sb("gtb%d" % i, [128, 1024]) for i in range(2)]
            wgt = sb("wgt", [128, 3, 8]); wmx = sb("wmx", [128, 8]); acc = [sb("accb%d" % i, [128, 1024]) for i in range(2)]
            def ldo(t):
                rows = slice(t * 128, (t + 1) * 128)
                for gi in range(3):
                    P.dma(ogt[t % 2].ap[:, gi], og_d.ap[gi, rows], r=[og_d], w=[ogt[t % 2]])
                pa_, pb_ = pc(rows, B_GATE, B_GATE + 1024)
                P.dma(gt[t % 2].ap[:], pa_, r=[pb_], w=[gt[t % 2]])
            if not kb_skip:
                ldo(0)
            for t in range(NT if not kb_skip else 0):
                rows = slice(t * 128, (t + 1) * 128)
                og = ogt[t % 2]; g_ = gt[t % 2]; ac = acc[t % 2]
                if t + 1 < NT:
                    ldo(t + 1)
                V("dve", "tensor_tensor", [og], [wmx], out=wmx.ap[:], in0=og.ap[:, 0, :, 128], in1=og.ap[:, 1, :, 128], op=ALU.max)
                V("dve", "tensor_tensor", [og, wmx], [wmx], out=wmx.ap[:], in0=wmx.ap[:], in1=og.ap[:, 2, :, 128], op=ALU.max)
                for gi in range(3):
                    V("dve", "tensor_tensor", [og, wmx], [wgt], out=wgt.ap[:, gi, :], in0=og.ap[:, gi, :, 128], in1=wmx.ap[:], op=ALU.subtract)
                V("act", "activation", [wgt], [wgt], out=wgt.ap[:], in_=wgt.ap[:], func=AF.Exp)
                V("dve", "tensor_tensor", [wgt], [wmx], out=wmx.ap[:], in0=wgt.ap[:, 0, :], in1=wgt.ap[:, 1, :], op=ALU.add)
                V("dve", "tensor_tensor", [wgt, wmx], [wmx], out=wmx.ap[:], in0=wmx.ap[:], in1=wgt.ap[:, 2, :], op=ALU.add)
                V("dve", "reciprocal", [wmx], [wmx], out=wmx.ap[:], in_=wmx.ap[:])
                for gi in range(3):
                    V("dve", "tensor_tensor", [wgt, wmx], [wgt], out=wgt.ap[:, gi, :], in0=wgt.ap[:, gi, :], in1=wmx.ap[:], op=ALU.mult)
                V("act", "activation", [g_], [g_], out=g_.ap[:], in_=g_.ap[:], func=AF.Silu)
                for j in range(8):
                    cs = slice(j * 128, (j + 1) * 128)
                    V("dve", "tensor_scalar", [og, wgt], [ac], out=ac.ap[:, cs], in0=og.ap[:, 0, j, 0:128], scalar1=wgt.ap[:, 0, j:j + 1], scalar2=None, op0=ALU.mult)
                    for gi in (1, 2):
                        V("dve", "scalar_tensor_tensor", [og, wgt, ac], [ac], out=ac.ap[:, cs], in0=og.ap[:, gi, j, 0:128], scalar=wgt.ap[:, gi, j:j + 1],
                          in1=ac.ap[:, cs], op0=ALU.mult, op1=ALU.add)
                V("pool", "tensor_tensor", [ac, g_], [ac], out=ac.ap[:], in0=ac.ap[:], in1=g_.ap[:], op=ALU.mult)
                P.dma(o_d.ap[rows, 2048:3072], ac.ap[:], r=[ac], w=[o_d])
        run_stage(s3b, "b")

        def s3a(sb):
            NS = 4
            cw = sb("cw", [128, 48, 4]); nea = sb("nea", [128, 16]); dtb = sb("dtb", [128, 16]); gon = sb("gon", [128, 128])
            P.dma(cw.ap[:], Wl["convT"][:, :, :], w=[cw]); P.dma(nea.ap[:], Wl["a_logB"][:, :], w=[nea])
            P.dma(dtb.ap[:], Wl["dt_biasB"][:, :], w=[dtb]); P.dma(gon.ap[:], Wl["g_onormB"][:, :], w=[gon])
            V("act", "activation", [nea], [nea], out=nea.ap[:], in_=nea.ap[:], func=AF.Exp)
            V("dve", "tensor_scalar", [nea], [nea], out=nea.ap[:], in0=nea.ap[:], scalar1=-1.0, scalar2=None, op0=ALU.mult)
            S = [sb("S%d" % h, [128, 128]) for h in range(16)]
            halo = [sb("halo%d" % h, [128, 3, 3]) for h in range(16)]
            for h in range(16):
                V("pool", "memset", [], [S[h]], ap=S[h].ap[:], constant=0.0)
                V("pool", "memset", [], [halo[h]], ap=halo[h].ap[:], constant=0.0)
            bas = [sb("ba%d" % i, [128, 32]) for i in range(2)]
            tks = [sb("tk%d" % i, [128, 8, 16]) for i in range(2)]
            rowsTs = [sb("rowsT%d" % i, [16, 256]) for i in range(2)]
            def mkalloc(b0, b1):
                st_ = [0]

                def al():
                    st_[0] += 1
                    return psb[b0] if st_[0] % 2 == 1 else psb[b1]
                return al
            slots = []
            for i in range(NS):
                slots.append(dict(psum=mkalloc(2 * i, 2 * i + 1),
                    a=sb("qkvg%d" % i, [128, 4, 128]), x_=sb("xc%d" % i, [128, 3, 131]), ac=sb("acc%d" % i, [128, 3, 128]),
                    sq=sb("sq%d" % i, [128, 256]), rn=sb("rn%d" % i, [128, 256]), qk=sb("qk_%d" % i, [128, 2, 128]),
                    tt=sb("tok%d" % i, [128, 3, 128]), d_=sb("dd%d" % i, [128, 4, 128]), cv=sb("cv%d" % i, [128, 128]),
                    qd_=sb("qd%d" % i, [128, 128]), qkT_=sb("qkT%d" % i, [128, 128]),
                    XY=[sb("XY%d_%d" % (i, k), [128, 2, 128]) for k in range(3)], U=[sb("U%d_%d" % (i, k), [128, 128]) for k in range(3)],
                    nk_=sb("nk%d" % i, [128, 128]), vn_=sb("vn%d" % i, [128, 128]), sg_=sb("sg%d" % i, [128, 128]),
                    rs_=sb("res%d" % i, [128, 128]), so=sb("so%d" % i, [128, 4])))

            def head(t, hh, B, tk, rowsT, rows):
                a = B["a"]; x_ = B["x_"]; ac = B["ac"]; sq = B["sq"]; rn = B["rn"]; qk = B["qk"]; tt = B["tt"]; d_ = B["d_"]; cv = B["cv"]
                qd_ = B["qd_"]; qkT_ = B["qkT_"]; XY = B["XY"]; U = B["U"]; nk_ = B["nk_"]; vn_ = B["vn_"]; sg_ = B["sg_"]; rs_ = B["rs_"]; so = B["so"]
                Sh = S[hh]; hl = halo[hh]
                psum = B["psum"]
                for s3 in range(3):
                    pa_, pb_ = pc(rows, s3 * 2048 + hh * 128, s3 * 2048 + (hh + 1) * 128)
                    P.dma(a.ap[:, s3, :], pa_, r=[pb_], w=[a])
                pa_, pb_ = pc(rows, A_GATE + hh * 128, A_GATE + (hh + 1) * 128)
                P.dma(a.ap[:, 3, :], pa_, r=[pb_], w=[a])
                ps = psum()
                for s in range(3):
                    tr(ps.ap[:, s * 128:(s + 1) * 128], a.ap[:, s, :], [a], [ps])
                yield
                V("pool", "tensor_copy", [hl], [x_], out=x_.ap[:, :, 0:3], in_=hl.ap[:])
                V("act", "activation", [ps], [x_], out=x_.ap[:, :, 3:131], in_=ps.ap[:, 0:384].rearrange("p (a b) -> p a b", a=3), func=AF.Identity)
                V("pool", "tensor_copy", [x_], [hl], out=hl.ap[:], in_=x_.ap[:, :, 128:131])
                V("act", "activation", [a], [sg_], out=sg_.ap[:], in_=a.ap[:, 3, :], func=AF.Silu)
                V("pool", "tensor_tensor", [sg_, gon], [sg_], out=sg_.ap[:], in0=sg_.ap[:], in1=gon.ap[:], op=ALU.mult)
                yield
                for s in range(3):
                    ci = s * 16 + hh
                    eng = "dve"
                    V("pool" if s == 2 else "dve", "tensor_scalar", [x_, cw], [ac], out=ac.ap[:, s, :], in0=x_.ap[:, s, 0:128], scalar1=cw.ap[:, ci, 0:1], scalar2=None, op0=ALU.mult)
                    for jj in range(1, 4):
                        if eng == "dve":
                            V("dve", "scalar_tensor_tensor", [x_, cw, ac], [ac], out=ac.ap[:, s, :], in0=x_.ap[:, s, jj:jj + 128], scalar=cw.ap[:, ci, jj:jj + 1],
                              in1=ac.ap[:, s, :], op0=ALU.mult, op1=ALU.add)
                        else:
                            V("pool", "tensor_scalar", [x_, cw], [cv], out=cv.ap[:], in0=x_.ap[:, s, jj:jj + 128], scalar1=cw.ap[:, ci, jj:jj + 1], scalar2=None, op0=ALU.mult)
                            V("pool", "tensor_tensor", [cv, ac], [ac], out=ac.ap[:, s, :], in0=ac.ap[:, s, :], in1=cv.ap[:], op=ALU.add)
                yield
                V("act", "activation", [ac], [ac], out=ac.ap[:], in_=ac.ap[:], func=AF.Silu)
                V("pool", "tensor_tensor", [ac], [sq], out=sq.ap[:], in0=ac.ap[:, 0:2, :].rearrange("p a b -> p (a b)"), in1=ac.ap[:, 0:2, :].rearrange("p a b -> p (a b)"), op=ALU.mult)
                sps = psum()
                mm(sps.ap[:, 0:256], ones.ap[:], sq.ap[:], [ones, sq], [sps])
                yield
                V("dve", "tensor_scalar", [sps], [rn], out=rn.ap[:], in0=sps.ap[:, 0:256], scalar1=EPS, scalar2=None, op0=ALU.add)
                V("act", "activation", [rn], [rn], out=rn.ap[:], in_=rn.ap[:], func=AF.Sqrt)
                V("dve", "reciprocal", [rn], [rn], out=rn.ap[:], in_=rn.ap[:])
                yield
                V("dve", "scalar_tensor_tensor", [ac, rn], [qk], out=qk.ap[:, 0, :], in0=ac.ap[:, 0, :], scalar=128.0 ** -0.5, in1=rn.ap[:, 0:128], op0=ALU.mult, op1=ALU.mult)
                V("pool", "tensor_tensor", [ac, rn], [qk], out=qk.ap[:, 1, :], in0=ac.ap[:, 1, :], in1=rn.ap[:, 128:256], op=ALU.mult)
                tps = psum()
                tr(tps.ap[:, 0:128], qk.ap[:, 1, :], [qk], [tps])
                tr(tps.ap[:, 128:256], ac.ap[:, 2, :], [ac], [tps])
                bps = tps
                mm(bps.ap[:, 256:384], sel.ap[:, hh * 128:(hh + 1) * 128], rowsT.ap[:, 0:128], [sel, rowsT], [bps])
                mm(bps.ap[:, 384:512], sel.ap[:, hh * 128:(hh + 1) * 128], rowsT.ap[:, 128:256], [sel, rowsT], [bps])
                kps = psum()
                mm(kps.ap[:, 0:128], qk.ap[:, 1, :], qk.ap[:, 1, :], [qk], [kps])
                mm(kps.ap[:, 128:256], qk.ap[:, 1, :], qk.ap[:, 0, :], [qk], [kps])
                yield
                V("dve", "tensor_scalar", [tps, tk], [tt], out=tt.ap[:, 0, :], in0=tps.ap[:, 0:128], scalar1=tk.ap[:, 6, hh:hh + 1], scalar2=None, op0=ALU.mult)
                V("dve", "tensor_scalar", [tps, tk], [tt], out=tt.ap[:, 1, :], in0=tps.ap[:, 0:128], scalar1=tk.ap[:, 4, hh:hh + 1], scalar2=None, op0=ALU.mult)
                V("dve", "tensor_scalar", [tps, tk], [tt], out=tt.ap[:, 2, :], in0=tps.ap[:, 128:256], scalar1=tk.ap[:, 0, hh:hh + 1], scalar2=None, op0=ALU.mult)
                V("dve", "tensor_scalar", [bps, tk], [d_], out=d_.ap[:, 0, :], in0=bps.ap[:, 256:384], scalar1=tk.ap[:, 2, hh:hh + 1], scalar2=0.0, op0=ALU.subtract, op1=ALU.min)
                yield
                V("act", "activation", [d_], [d_], out=d_.ap[:, 0, :], in_=d_.ap[:, 0, :], func=AF.Exp)
                V("act", "activation", [bps], [d_], out=d_.ap[:, 3, :], in_=bps.ap[:, 256:384], func=AF.Exp)
                yield
                V("pool", "tensor_tensor", [d_, ltri], [d_], out=d_.ap[:, 1, :], in0=d_.ap[:, 0, :], in1=ltri.ap[:], op=ALU.mult)
                V("dve", "scalar_tensor_tensor", [bps, d_], [d_], out=d_.ap[:, 2, :], in0=bps.ap[:, 384:512], scalar=-1.0, in1=d_.ap[:, 0, :], op0=ALU.mult, op1=ALU.mult)
                V("pool", "tensor_tensor", [d_, lstr], [d_], out=d_.ap[:, 2, :], in0=d_.ap[:, 2, :], in1=lstr.ap[:], op=ALU.mult)
                V("pool", "tensor_tensor", [qk, d_], [qd_], out=qd_.ap[:], in0=qk.ap[:, 0, :], in1=d_.ap[:, 3, :], op=ALU.mult)
                yield
                X = XY[0]
                V("dve", "tensor_tensor", [kps, d_], [X], out=X.ap[:, 0, :], in0=kps.ap[:, 0:128], in1=d_.ap[:, 2, :], op=ALU.mult)
                V("dve", "tensor_tensor", [kps, d_], [qkT_], out=qkT_.ap[:], in0=kps.ap[:, 128:256], in1=d_.ap[:, 1, :], op=ALU.mult)
                yps = psum()
                tr(yps.ap[:, 0:128], X.ap[:, 0, :], [X], [yps])
                u_ = U[0]
                V("pool", "tensor_tensor", [X, ident], [u_], out=u_.ap[:], in0=X.ap[:, 0, :], in1=ident.ap[:], op=ALU.add)
                yield
                V("act", "activation", [yps], [X], out=X.ap[:, 1, :], in_=yps.ap[:, 0:128], func=AF.Identity)
                yield
                for s in range(6):
                    Xn = XY[(s + 1) % 3]
                    xps = psum()
                    mm(xps.ap[:, 0:128], X.ap[:, 1, :], X.ap[:, 0, :], [X], [xps])
                    mm(xps.ap[:, 128:256], X.ap[:, 0, :], X.ap[:, 1, :], [X], [xps])
                    yield
                    V("act", "activation", [xps], [Xn], out=Xn.ap[:], in_=xps.ap[:, 0:256].rearrange("p (a b) -> p a b", a=2), func=AF.Identity)
                    ups = psum()
                    mm(ups.ap[:, 0:128], Xn.ap[:, 1, :], u_.ap[:], [Xn, u_], [ups])
                    yield
                    un = U[(s + 1) % 3]
                    V("dve", "tensor_tensor", [ups, u_], [un], out=un.ap[:], in0=ups.ap[:, 0:128], in1=u_.ap[:], op=ALU.add)
                    X = Xn; u_ = un
                cps = psum()
                mm(cps.ap[:, 0:128], tt.ap[:, 0, :], u_.ap[:], [tt, u_], [cps])
                yield
                V("act", "activation", [cps], [nk_], out=nk_.ap[:], in_=cps.ap[:, 0:128], func=AF.Copy, scale=-1.0)
                vps = psum()
                mm(vps.ap[:, 0:128], u_.ap[:], tt.ap[:, 2, :], [u_, tt], [vps], start=True, stop=False)
                mm(vps.ap[:, 0:128], nk_.ap[:], Sh.ap[:], [nk_, Sh], [vps], start=False, stop=True)
                yield
                V("act", "activation", [vps], [vn_], out=vn_.ap[:], in_=vps.ap[:, 0:128], func=AF.Identity)
                ops_ = psum()
                mm(ops_.ap[:, 0:128], qd_.ap[:], Sh.ap[:], [qd_, Sh], [ops_], start=True, stop=False)
                mm(ops_.ap[:, 0:128], qkT_.ap[:], vn_.ap[:], [qkT_, vn_], [ops_], start=False, stop=True)
                sps2 = psum()
                mm(sps2.ap[:, 0:128], tt.ap[:, 1, :], vn_.ap[:], [tt, vn_], [sps2])
                yield
                V("dve", "scalar_tensor_tensor", [Sh, tk, sps2], [Sh], out=Sh.ap[:], in0=Sh.ap[:], scalar=tk.ap[:, 5, hh:hh + 1], in1=sps2.ap[:, 0:128],
                  op0=ALU.mult, op1=ALU.add)
                V("act", "activation", [ops_], [rs_, so], out=rs_.ap[:], in_=ops_.ap[:, 0:128], func=AF.Square, accum_out=so.ap[:, 0:1])
                yield
                V("dve", "tensor_scalar", [so], [so], out=so.ap[:, 1:2], in0=so.ap[:, 0:1], scalar1=1.0 / 128, scalar2=EPS, op0=ALU.mult, op1=ALU.add)
                V("act", "activation", [so], [so], out=so.ap[:, 1:2], in_=so.ap[:, 1:2], func=AF.Sqrt)
                yield
                V("dve", "reciprocal", [so], [so], out=so.ap[:, 2:3], in_=so.ap[:, 1:2])
                V("dve", "scalar_tensor_tensor", [ops_, so, sg_], [rs_], out=rs_.ap[:], in0=ops_.ap[:, 0:128], scalar=so.ap[:, 2:3], in1=sg_.ap[:], op0=ALU.mult, op1=ALU.mult)
                P.dma(o_d.ap[rows, hh * 128:(hh + 1) * 128], rs_.ap[:], r=[rs_], w=[o_d])

            def prologue(t):
                rows = slice(t * 128, (t + 1) * 128)
                ba = bas[t % 2]; tk = tks[t % 2]; rowsT = rowsTs[t % 2]
                pa_, pb_ = pc(rows, A_BETA, A_BETA + 32)
                P.dma(ba.ap[:], pa_, r=[pb_], w=[ba])
                V("act", "activation", [ba], [tk], out=tk.ap[:, 0, :], in_=ba.ap[:, 0:16], func=AF.Sigmoid)
                V("dve", "tensor_tensor", [ba, dtb], [tk], out=tk.ap[:, 7, :], in0=ba.ap[:, 16:32], in1=dtb.ap[:], op=ALU.add)
                V("act", "activation", [tk], [tk], out=tk.ap[:, 7, :], in_=tk.ap[:, 7, :], func=AF.Exp)
                V("act", "activation", [tk], [tk], out=tk.ap[:, 7, :], in_=tk.ap[:, 7, :], func=AF.Ln, bias=1.0)
                V("dve", "tensor_tensor", [tk, nea], [tk], out=tk.ap[:, 1, :], in0=tk.ap[:, 7, :], in1=nea.ap[:], op=ALU.mult)
                gps = slots[0]["psum"]()
                mm(gps.ap[:, 0:16], ltri.ap[:], tk.ap[:, 1, :], [ltri, tk], [gps])
                mm(gps.ap[:, 16:32], ones.ap[:], tk.ap[:, 1, :], [ones, tk], [gps])
                V("dve", "tensor_copy", [gps], [tk], out=tk.ap[:, 2, :], in_=gps.ap[:, 0:16])
                V("act", "activation", [gps], [tk], out=tk.ap[:, 3, :], in_=gps.ap[:, 0:16], func=AF.Exp)
                V("dve", "tensor_tensor", [gps, tk], [tk], out=tk.ap[:, 4, :], in0=gps.ap[:, 16:32], in1=tk.ap[:, 2, :], op=ALU.subtract)
                V("act", "activation", [tk], [tk], out=tk.ap[:, 4, :], in_=tk.ap[:, 4, :], func=AF.Exp)
                V("act", "activation", [gps], [tk], out=tk.ap[:, 5, :], in_=gps.ap[:, 16:32], func=AF.Exp)
                V("dve", "tensor_tensor", [tk], [tk], out=tk.ap[:, 6, :], in0=tk.ap[:, 0, :], in1=tk.ap[:, 3, :], op=ALU.mult)
                rps = slots[0]["psum"]()
                tr(rps.ap[0:16, 0:128], tk.ap[:, 2, :], [tk], [rps])
                tr(rps.ap[0:16, 128:256], tk.ap[:, 0, :], [tk], [rps])
                V("act", "activation", [rps], [rowsT], out=rowsT.ap[:], in_=rps.ap[0:16, 0:256], func=AF.Identity)

            tasks = [(t, hh) for t in range(NT) for hh in range(16)]
            tit = iter(tasks)
            active = [None] * NS
            done = False
            pro_done = -1
            while True:
                for i in range(NS):
                    if active[i] is None and not done:
                        try:
                            t, hh = next(tit)
                            while pro_done < min(t + (1 if hh >= 8 else 0), NT - 1):
                                pro_done += 1
                                prologue(pro_done)
                            active[i] = head(t, hh, slots[i], tks[t % 2], rowsTs[t % 2], slice(t * 128, (t + 1) * 128))
                        except StopIteration:
                            done = True
                    if active[i] is not None:
                        try:
                            next(active[i])
                        except StopIteration:
                            active[i] = None
                if done and all(a_ is None for a_ in active):
                    break
        run_stage(s3a, "a")

        def s4a(sb):
            ots = [sb("ot%d" % i, [128, 5120]) for i in range(2)]
            oss = [sb("os%d" % i, [128, 40, 128], BF16) for i in range(2)]
            def ldt(t):
                P.dma(ots[t % 2].ap[:], o_d.ap[t * 128:(t + 1) * 128, :], r=[o_d], w=[ots[t % 2]])
            ldt(0)
            for t in range(NT):
                ot = ots[t % 2]; os_ = oss[t % 2]
                if t + 1 < NT:
                    ldt(t + 1)
                for c4 in range(10):
                    pt = psum()
                    for i in range(4):
                        c = c4 * 4 + i
                        tr(pt.ap[:, i * 128:(i + 1) * 128], ot.ap[:, c * 128:(c + 1) * 128], [ot], [pt])
                    if c4 % 2 == 0:
                        V("act", "activation", [pt], [os_], out=os_.ap[:, c4 * 4:c4 * 4 + 4, :], in_=pt.ap[:].rearrange("p (a b) -> p a b", a=4), func=AF.Identity)
                    else:
                        V("dve", "tensor_copy", [pt], [os_], out=os_.ap[:, c4 * 4:c4 * 4 + 4, :], in_=pt.ap[:].rearrange("p (a b) -> p a b", a=4))
                P.dma(oT_d.ap[t], os_.ap[:], r=[os_], w=[oT_d])
        run_stage(s4a, "t")

        def s4b(sb):
            wts = [sb("wbt%d" % i, [128, 40, 512], BF16) for i in range(2)]
            ats = [sb("abt%d" % i, [128, 40, 128], BF16) for i in range(4)]
            gts = [sb("gts%d" % i, [128, 3, 512]) for i in range(3)]
            ysb = [sb("ysb%d" % i, [128, 512]) for i in range(2)]
            ysT = [sb("ysT%d" % i, [128, 4, 128], BF16) for i in range(2)]
            wvv = Wl["w_br"].rearrange("(c p) n -> p c n", p=128)
            branches = ((0, 16), (16, 8), (24, 16))
            work = [(nb, t) for nb in range(8) for t in range(NT)]
            PF = 2

            def load_w(nb):
                wt = wts[nb % 2]
                for k4 in range(0, 40, 8):
                    P.dma(wt.ap[:, k4:k4 + 8, :], wvv[:, k4:k4 + 8, nb * 512:(nb + 1) * 512], w=[wt], q="pool")

            def load_a(i):
                nb, t = work[i]
                at = ats[i % 4]; g = gts[i % 3]
                rows = slice(t * 128, (t + 1) * 128)
                P.dma(at.ap[:], oT_d.ap[t], r=[oT_d], w=[at])
                for bi in range(3):
                    c0 = MERGE + bi * D + nb * 512
                    pa_, pb_ = pc(rows, c0, c0 + 512)
                    P.dma(g.ap[:, bi, :], pa_, r=[pb_], w=[g])

            load_w(0)
            for i in range(min(PF, len(work))):
                load_a(i)
            for i, (nb, t) in enumerate(work):
                wt = wts[nb % 2]
                if t == 0 and nb + 1 < 8:
                    load_w(nb + 1)
                if i + PF < len(work):
                    load_a(i + PF)
                at = ats[i % 4]; g = gts[i % 3]; y = ysb[i % 2]; yt = ysT[i % 2]
                V("act", "activation", [g], [g], out=g.ap[:], in_=g.ap[:], func=AF.Sigmoid)
                pss = []
                for bi, (k0, kn) in enumerate(branches):
                    ps = psum(); pss.append(ps)
                    for k in range(kn):
                        mm(ps.ap[:], at.ap[:, k0 + k, :], wt.ap[:, k0 + k, :], [at, wt], [ps], start=(k == 0), stop=(k == kn - 1))
                V("dve", "tensor_tensor", [pss[0], g], [y], out=y.ap[:], in0=pss[0].ap[:], in1=g.ap[:, 0, :], op=ALU.mult)
                V("dve", "tensor_tensor", [pss[1], g], [g], out=g.ap[:, 1, :], in0=pss[1].ap[:], in1=g.ap[:, 1, :], op=ALU.mult)
                V("dve", "tensor_tensor", [pss[2], g], [g], out=g.ap[:, 2, :], in0=pss[2].ap[:], in1=g.ap[:, 2, :], op=ALU.mult)
                V("pool", "tensor_tensor", [y, g], [y], out=y.ap[:], in0=y.ap[:], in1=g.ap[:, 1, :], op=ALU.add)
                V("pool", "tensor_tensor", [y, g], [y], out=y.ap[:], in0=y.ap[:], in1=g.ap[:, 2, :], op=ALU.add)
                pt = psum()
                for j in range(4):
                    tr(pt.ap[:, j * 128:(j + 1) * 128], y.ap[:, j * 128:(j + 1) * 128], [y], [pt])
                V("act", "activation", [pt], [yt], out=yt.ap[:], in_=pt.ap[:].rearrange("p (a b) -> p a b", a=4), func=AF.Identity)
                P.dma(yT_d.ap[t][:, nb * 4:nb * 4 + 4, :], yt.ap[:], r=[yt], w=[yT_d])
        run_stage(s4b, "m")

        if dbg is None or "o" in dbg:
            gemm_stage(yT_d, KC, Wl["w_out"], D, ob_d, "o")

        def s4d(sb):
            obv = ob_d.ap.rearrange("(n p) d -> n p d", p=128)
            nxv = nxt_x.ap.rearrange("(n p) d -> n p d", p=128)
            ots = [sb("fo%d" % i, [128, D]) for i in range(2)]
            xts = [sb("fx%d" % i, [128, D]) for i in range(2)]
            junk = sb("fjunk", [128, D])
            def ldf(t):
                P.dma(ots[t % 2].ap[:], obv[t], r=[ob_d], w=[ots[t % 2]])
                P.dma(xts[t % 2].ap[:], xv[t], r=[cur_x], w=[xts[t % 2]])
            ldf(0)
            for t in range(NT):
                ot = ots[t % 2]; xt = xts[t % 2]
                if t + 1 < NT:
                    ldf(t + 1)
                V("act", "activation", [ot], [junk, ss], out=junk.ap[:], in_=ot.ap[:], func=AF.Square, accum_out=ss.ap[:, 0:1])
                rsqrt_to(rs, ss, 1.0 / D, 1)
                V("dve", "scalar_tensor_tensor", [ot, rs, ggB], [ot], out=ot.ap[:], in0=ot.ap[:], scalar=rs.ap[:, 0:1], in1=ggB.ap[:], op0=ALU.mult, op1=ALU.mult)
                V("pool", "tensor_tensor", [ot, xt], [ot], out=ot.ap[:], in0=ot.ap[:], in1=xt.ap[:], op=ALU.add)
                P.dma(nxv[t], ot.ap[:], r=[ot], w=[nxt_x])
        run_stage(s4d, "f")
        cur_x = nxt_x

    stack.close()
    return nc


def _consts():
    ident = np.eye(128, dtype=np.float32)
    s = np.arange(128)
    ltri = (s[:, None] <= s[None, :]).astype(np.float32)
    lstr = (s[:, None] < s[None, :]).astype(np.float32)
    sel = np.zeros((16, 16, 128), np.float32)
    for h in range(16):
        sel[h, h, :] = 1.0
    slopes = 2.0 ** (-8.0 * np.arange(1, 25, dtype=np.float32) / 24)
    i = np.arange(128)[:, None]
    j = np.arange(256)[None, :]
    rel = i + 128 - j
    valid = (rel >= 0) & (rel <= 128)
    bm = np.zeros((24, 128, 256), np.float32)
    for h in range(24):
        d = (1, 4, 16)[h // 8]
        bm[h] = np.where(valid, -slopes[h] * (d * rel).astype(np.float32), NEG)
    return dict(ident=ident, ltri=ltri, lstr=lstr, sel=sel.reshape(16, 2048), bmask=bm)


def _layer_inputs(inp, l):
    f = np.float32
    def fm(v, n):
        return np.ascontiguousarray(v.reshape(n, 128).T).astype(f)
    def bc(v):
        return np.ascontiguousarray(np.broadcast_to(v[None, :], (128, v.shape[0]))).astype(f)
    b_ada = inp["b_ada"][l]
    conv = inp["conv_a"][l]
    convT = np.ascontiguousarray(conv.reshape(4, 48, 128).transpose(2, 1, 0))
    w_br = np.concatenate([inp["w_br_a"][l], inp["w_br_b"][l], inp["w_br_c"][l]], axis=0)
    return {
        "w_ada%d" % l: inp["w_ada"][l], "b_adaT%d" % l: fm(b_ada[0:8192], 64), "b_gateB%d" % l: bc(b_ada[8192:]),
        "g_preT%d" % l: fm(inp["g_pre"][l], 32), "g_postB%d" % l: bc(inp["g_post"][l]),
        "w_in%d" % l: inp["w_in"][l], "convT%d" % l: convT,
        "a_logB%d" % l: bc(inp["a_log"][l]), "dt_biasB%d" % l: bc(inp["dt_bias"][l]),
        "g_onormB%d" % l: bc(inp["g_onorm_a"][l]),
        "ln_gB%d" % l: bc(inp["ln_c_g"][l]), "ln_bB%d" % l: bc(inp["ln_c_b"][l]),
        "w_spT%d" % l: np.ascontiguousarray(inp["w_spatial"][l].transpose(2, 0, 1)),
        "b_spT%d" % l: np.ascontiguousarray(inp["b_spatial"][l].T),
        "w_br%d" % l: w_br, "w_out%d" % l: inp["w_out"][l],
    }


def kernel(**inputs):
    inp = {k: np.asarray(v) for k, v in inputs.items()}
    x = inp["x"]
    B, T, _ = x.shape
    L = inp["w_in"].shape[0]
    nc = build(T, L)
    shared = _consts()
    for l in range(L):
        shared.update(_layer_inputs(inp, l))
    in_maps = []
    for b in range(B):
        m = dict(shared)
        m["x"] = np.ascontiguousarray(x[b])
        m["cT"] = np.ascontiguousarray(inp["c"][b].reshape(KC, 128).T)
        in_maps.append(m)
    res = run_bass_kernel_spmd(nc, in_maps, core_ids=list(range(B)))
    return np.stack([res.results[b]["out"] for b in range(B)], axis=0).astype(np.float32)
```
